# Optimizing a Trainium2 kernel written in Bass

```python
import math
import jax, jax.numpy as jnp
from jax import lax
import numpy as np

D_MODEL = 4096
BATCH = 4
SEQ = 2048
DEPTH = 2

GRID_W = 64
CTX_LEN = 256
FOURIER_WIDTH = 2048
FOURIER_GROUPS = 4
FOURIER_GROUP_DIM = FOURIER_WIDTH // FOURIER_GROUPS
SSM_WIDTH = 2048
SSM_GROUP_DIM = 16
SSM_GROUPS = SSM_WIDTH // SSM_GROUP_DIM
SSM_STATE = 64
SCAN_DIRECTIONS = (False, True)
FFN_DIM = 11008
CONV_WIDTH = 3
N_MOD = 6
IN_SPLITS = (FOURIER_WIDTH, FOURIER_WIDTH + SSM_WIDTH, FOURIER_WIDTH + SSM_WIDTH + D_MODEL)
IN_COLS = FOURIER_WIDTH + SSM_WIDTH + 2 * D_MODEL
DEEPNORM_ALPHA = (2.0 * DEPTH) ** 0.25
DEEPNORM_BETA = (8.0 * DEPTH) ** -0.25
LN_EPS = 1e-6
DT_MIN = 1e-3
DT_MAX = 1e-1

kernel_name = "hybrid_fourier_s5_convffn_dit_block"


def layer_norm(x):
    xf = x.astype(jnp.float32)
    mu = jnp.mean(xf, axis=-1, keepdims=True)
    var = jnp.mean(jnp.square(xf - mu), axis=-1, keepdims=True)
    return (xf - mu) * lax.rsqrt(var + LN_EPS)


def post_norm(res, out, g, b):
    y = layer_norm(DEEPNORM_ALPHA * res + out) * g + b
    return y.astype(res.dtype)


def modulate(x, shift, scale):
    return (layer_norm(x) * (1 + scale) + shift).astype(x.dtype)


def adaln(cond, w, b, n):
    m = jax.nn.silu(cond) @ w[:, : n * D_MODEL] + b[: n * D_MODEL]
    return jnp.split(m, n, axis=-1)


def fourier_mix(u):
    bn, length, _ = u.shape
    ug = u.astype(jnp.float32).reshape(bn, length, FOURIER_GROUPS, FOURIER_GROUP_DIM)
    f = jnp.fft.fft2(ug, axes=(1, 3), norm="ortho").real
    return f.reshape(bn, length, FOURIER_WIDTH).astype(u.dtype)


def zoh_discretize(a_re, a_im, log_dt, b_re, b_im):
    lam = lax.complex(a_re.astype(jnp.float32), a_im.astype(jnp.float32))
    dt = jnp.exp(log_dt.astype(jnp.float32))[:, None]
    a_bar = jnp.exp(lam * dt)
    b_mat = lax.complex(b_re.astype(jnp.float32), b_im.astype(jnp.float32))
    b_bar = ((a_bar - 1.0) / lam)[..., None] * b_mat
    return a_bar, b_bar


def ssm_combine(left, right):
    a_l, b_l = left
    a_r, b_r = right
    return a_r * a_l, a_r * b_l + b_r


def ssm_states(u, h0s, p):
    bn, length, _ = u.shape
    u_t = jnp.moveaxis(u.astype(jnp.float32).reshape(bn, length, SSM_GROUPS, SSM_GROUP_DIM), 1, 0)
    u_c = u_t.astype(jnp.complex64)
    states = []
    for d, reverse in enumerate(SCAN_DIRECTIONS):
        a_bar, b_bar = zoh_discretize(p["ssm_a_re"][d], p["ssm_a_im"][d], p["ssm_log_dt"][d],
                                      p["ssm_b_re"][d], p["ssm_b_im"][d])
        bu = jnp.einsum("gpc,lbgc->lbgp", b_bar, u_c)
        if h0s is not None:
            first = length - 1 if reverse else 0
            bu = bu.at[first].add(a_bar * h0s[d])
        a_seq = jnp.broadcast_to(a_bar, (length, 1) + a_bar.shape)
        _, h = lax.associative_scan(ssm_combine, (a_seq, bu), axis=0, reverse=reverse)
        states.append(h)
    return states


def ssm_final_states(states):
    return (states[0][-1], states[1][0])


def ssm_readout(states, u, p):
    bn, length, _ = u.shape
    y = p["ssm_d"].astype(jnp.float32) * u.astype(jnp.float32)
    for d, h in enumerate(states):
        c_mat = lax.complex(p["ssm_c_re"][d].astype(jnp.float32), p["ssm_c_im"][d].astype(jnp.float32))
        y = y + jnp.einsum("gcp,lbgp->blgc", c_mat, h).real.reshape(bn, length, SSM_WIDTH)
    y = jax.nn.gelu(y).astype(u.dtype)
    val, gate = jnp.split(y @ p["glu_w"], 2, axis=-1)
    return val * jax.nn.sigmoid(gate)


def token_mixer(h, h0s, p):
    proj = h @ p["w_in"]
    u_f, u_s, g_f, g_s = jnp.split(proj, IN_SPLITS, axis=-1)
    y_f = fourier_mix(u_f) @ p["fourier_w"]
    states = ssm_states(u_s, h0s, p)
    y_s = ssm_readout(states, u_s, p)
    merged = jax.nn.sigmoid(g_f) * y_f + jax.nn.sigmoid(g_s) * y_s
    return merged @ p["w_out"], states


def dwconv_centred(u, w, b, axis):
    length = u.shape[axis]
    pad = [(0, 0)] * u.ndim
    pad[axis] = (CONV_WIDTH // 2, CONV_WIDTH // 2)
    up = jnp.pad(u, pad)
    out = b
    for k in range(CONV_WIDTH):
        out = out + w[k] * lax.slice_in_dim(up, k, k + length, axis=axis)
    return out


def conv_ffn(h, p, grid_rows):
    u, v = jnp.split(h @ p["ffn_w12"], 2, axis=-1)
    if grid_rows is None:
        u = dwconv_centred(u, p["ffn_conv_w"], p["ffn_conv_b"], axis=1)
    else:
        bn, length, f = u.shape
        u = dwconv_centred(u.reshape(bn, grid_rows, GRID_W, f), p["ffn_conv_w"], p["ffn_conv_b"],
                           axis=2).reshape(bn, length, f)
    return (jax.nn.gelu(u) * v) @ p["ffn_w2"]


def setup_inputs(seed: int = 0) -> dict:
    key = jax.random.key(seed)
    ks = jax.random.split(key, 26)
    f32 = jnp.float32

    def nrm(k, shape, scale):
        return scale * jax.random.normal(k, shape, f32)

    G, P, CG = SSM_GROUPS, SSM_STATE, SSM_GROUP_DIM
    a_im0 = jnp.pi * jnp.arange(P, dtype=f32)
    return {
        "x": nrm(ks[0], (BATCH, SEQ, D_MODEL), 1.0),
        "c": nrm(ks[1], (BATCH, D_MODEL), 1.0),
        "ctx": nrm(ks[2], (BATCH, CTX_LEN, D_MODEL), 1.0),
        "c_ctx": nrm(ks[3], (D_MODEL,), 1.0),
        "ada_w": nrm(ks[4], (DEPTH, D_MODEL, N_MOD * D_MODEL), 0.5 * D_MODEL ** -0.5),
        "ada_b": nrm(ks[5], (DEPTH, N_MOD * D_MODEL), 0.01),
        "w_in": nrm(ks[6], (DEPTH, D_MODEL, IN_COLS), D_MODEL ** -0.5),
        "fourier_w": nrm(ks[7], (DEPTH, FOURIER_WIDTH, D_MODEL), FOURIER_WIDTH ** -0.5),
        "ssm_a_re": -0.5 + nrm(ks[8], (DEPTH, 2, G, P), 0.01),
        "ssm_a_im": a_im0 + nrm(ks[9], (DEPTH, 2, G, P), 0.01),
        "ssm_log_dt": jax.random.uniform(ks[10], (DEPTH, 2, G), f32, math.log(DT_MIN), math.log(DT_MAX)),
        "ssm_b_re": nrm(ks[11], (DEPTH, 2, G, P, CG), (2 * CG) ** -0.5),
        "ssm_b_im": nrm(ks[12], (DEPTH, 2, G, P, CG), (2 * CG) ** -0.5),
        "ssm_c_re": nrm(ks[13], (DEPTH, 2, G, CG, P), (2 * P) ** -0.5),
        "ssm_c_im": nrm(ks[14], (DEPTH, 2, G, CG, P), (2 * P) ** -0.5),
        "ssm_d": nrm(ks[15], (DEPTH, SSM_WIDTH), 1.0),
        "glu_w": nrm(ks[16], (DEPTH, SSM_WIDTH, 2 * D_MODEL), SSM_WIDTH ** -0.5),
        "w_out": nrm(ks[17], (DEPTH, D_MODEL, D_MODEL), DEEPNORM_BETA * D_MODEL ** -0.5),
        "ln1_g": 1.0 + nrm(ks[18], (DEPTH, D_MODEL), 0.01),
        "ln1_b": nrm(ks[19], (DEPTH, D_MODEL), 0.01),
        "ffn_w12": nrm(ks[20], (DEPTH, D_MODEL, 2 * FFN_DIM), D_MODEL ** -0.5),
        "ffn_conv_w": nrm(ks[21], (DEPTH, CONV_WIDTH, FFN_DIM), CONV_WIDTH ** -0.5),
        "ffn_conv_b": nrm(ks[22], (DEPTH, FFN_DIM), 0.01),
        "ffn_w2": nrm(ks[23], (DEPTH, FFN_DIM, D_MODEL), DEEPNORM_BETA * FFN_DIM ** -0.5),
        "ln2_g": 1.0 + nrm(ks[24], (DEPTH, D_MODEL), 0.01),
        "ln2_b": nrm(ks[25], (DEPTH, D_MODEL), 0.01),
    }


def reference(x, c, ctx, c_ctx, ada_w, ada_b, w_in, fourier_w, ssm_a_re, ssm_a_im, ssm_log_dt,
              ssm_b_re, ssm_b_im, ssm_c_re, ssm_c_im, ssm_d, glu_w, w_out, ln1_g, ln1_b,
              ffn_w12, ffn_conv_w, ffn_conv_b, ffn_w2, ln2_g, ln2_b):
    ROWS = x.shape[1] // GRID_W
    cond_x = c[:, None, :]
    cond_c = c_ctx[None, None, :]
    for i in range(DEPTH):
        p = {
            "ada_w": ada_w[i], "ada_b": ada_b[i], "w_in": w_in[i], "fourier_w": fourier_w[i],
            "ssm_a_re": ssm_a_re[i], "ssm_a_im": ssm_a_im[i], "ssm_log_dt": ssm_log_dt[i],
            "ssm_b_re": ssm_b_re[i], "ssm_b_im": ssm_b_im[i], "ssm_c_re": ssm_c_re[i],
            "ssm_c_im": ssm_c_im[i], "ssm_d": ssm_d[i], "glu_w": glu_w[i], "w_out": w_out[i],
            "ffn_w12": ffn_w12[i], "ffn_conv_w": ffn_conv_w[i], "ffn_conv_b": ffn_conv_b[i],
            "ffn_w2": ffn_w2[i],
        }
        last = i == DEPTH - 1
        mx = adaln(cond_x, p["ada_w"], p["ada_b"], N_MOD)
        mc = adaln(cond_c, p["ada_w"], p["ada_b"], 2 if last else N_MOD)

        hc = modulate(ctx, mc[0], mc[1])
        if last:
            u_s_ctx = hc @ p["w_in"][:, IN_SPLITS[0]:IN_SPLITS[1]]
            ctx_states = ssm_states(u_s_ctx, None, p)
        else:
            out_c, ctx_states = token_mixer(hc, None, p)
            ctx_mid = post_norm(ctx, mc[2] * out_c, ln1_g[i], ln1_b[i])
        h0s = ssm_final_states(ctx_states)

        hx = modulate(x, mx[0], mx[1])
        out_x, _ = token_mixer(hx, h0s, p)
        x = post_norm(x, mx[2] * out_x, ln1_g[i], ln1_b[i])
        x = post_norm(x, mx[5] * conv_ffn(modulate(x, mx[3], mx[4]), p, ROWS), ln2_g[i], ln2_b[i])

        if not last:
            ctx = post_norm(ctx_mid, mc[5] * conv_ffn(modulate(ctx_mid, mc[3], mc[4]), p, None),
                            ln2_g[i], ln2_b[i])
    return x
```

```python
import os
import math
import numpy as np
from contextlib import ExitStack
import concourse.bass as bass
import concourse.mybir as mybir
from concourse.bass_utils import run_bass_kernel_spmd

F32 = mybir.dt.float32
BF16 = mybir.dt.bfloat16
AF = mybir.ActivationFunctionType
ALU = mybir.AluOpType

D = 4096
KC = 32
LX = 2048
LC = 256
NT = LX + LC
DEPTH = 2
FFN = 11008
HB = FFN // 128
IN_COLS = 12288
ALPHA = (2.0 * DEPTH) ** 0.25
EPS = 1e-6
MAGIC = 12582912.0
TWO_PI_S = 6.28318
TILES = [(0, 512, 0), (512, 512, 0), (1024, 512, 0), (1536, 512, 0), (2048, 256, 1)]

WSHAPES = {
    "ada_w": [DEPTH, D, 6 * D], "ada_b": [DEPTH, 6 * D], "w_in": [DEPTH, D, IN_COLS],
    "fourier_w": [DEPTH, 2048, D], "ssm_a_re": [DEPTH, 2, 128, 64], "ssm_a_im": [DEPTH, 2, 128, 64],
    "ssm_log_dt": [DEPTH, 2, 128], "ssm_b_re": [DEPTH, 2, 128, 64, 16], "ssm_b_im": [DEPTH, 2, 128, 64, 16],
    "ssm_c_re": [DEPTH, 2, 128, 16, 64], "ssm_c_im": [DEPTH, 2, 128, 16, 64], "ssm_d": [DEPTH, 2048],
    "glu_w": [DEPTH, 2048, 2 * D], "w_out": [DEPTH, D, D], "ln1_g": [DEPTH, D], "ln1_b": [DEPTH, D],
    "ffn_w12": [DEPTH, D, 2 * FFN], "ffn_conv_w": [DEPTH, 3, FFN], "ffn_conv_b": [DEPTH, FFN],
    "ffn_w2": [DEPTH, FFN, D], "ln2_g": [DEPTH, D], "ln2_b": [DEPTH, D],
}


class Ctx:
    def __init__(self, nc):
        self.nc = nc
        self.es = ExitStack()
        self.S = {}
        self.bank_uses = [0] * 8
        self.uses = {}
        self.free = {}
        self._inced = set()
        self.engines = [nc.sync, nc.scalar, nc.vector, nc.gpsimd, nc.tensor]

    def sem(self, name):
        if name not in self.S:
            self.S[name] = [self.es.enter_context(self.nc.semaphore(name)), 0]
        return self.S[name]

    def inc(self, ins, name, n=1):
        assert id(ins) not in self._inced, "instruction already has a then_inc"
        self._inced.add(id(ins))
        self._keep = getattr(self, "_keep", [])
        self._keep.append(ins)
        e = self.sem(name)
        ins.then_inc(e[0], n)
        e[1] += n
        return e[1]

    def cnt(self, name):
        return self.sem(name)[1]

    def evt(self, name):
        return (name, self.sem(name)[1])

    def wait_evt(self, eng, ev):
        if ev is None:
            return
        if isinstance(ev, list):
            for e in ev:
                self.wait_evt(eng, e)
            return
        self.wait(eng, ev[0], ev[1])

    def bank_evt(self, b):
        return (f"br{b}", self.bank_uses[b])

    def wait(self, eng, name, val=None):
        e = self.sem(name)
        v = e[1] if val is None else val
        if v > 0:
            eng.wait_ge(e[0], v)

    def barrier(self):
        for eng in self.engines:
            for name, (h, c) in self.S.items():
                if c > 0:
                    eng.wait_ge(h, c)

    def use(self, key):
        k = self.uses.get(key, 0)
        self.uses[key] = k + 1
        return k

    def bank_begin(self, b):
        k = self.bank_uses[b]
        self.wait(self.nc.tensor, f"bf{b}", k)

    def bank_ready(self, ins, b):
        self.inc(ins, f"br{b}")
        self.bank_uses[b] += 1

    def bank_wait(self, eng, b):
        self.wait(eng, f"br{b}", self.bank_uses[b])

    def bank_free(self, ins, b):
        self.inc(ins, f"bf{b}")


def build(debug_outs=(), stop_after=None):
    nc = bass.Bass("TRN2", target_bir_lowering=False)
    C = Ctx(nc)
    PE, ACT, DVE, POOL, SP = nc.tensor, nc.scalar, nc.vector, nc.gpsimd, nc.sync

    def din(name, shape):
        return nc.dram_tensor(name, shape, F32, kind="ExternalInput").ap()

    xb = din("xb", [LX, D])
    ctxb = din("ctxb", [LC, D])
    cvec = din("cvec", [2, D])
    W = {k: din(k, s) for k, s in WSHAPES.items()}
    ident_d = din("ident_in", [128, 128])
    cos_d = din("cosT", [2048, 2048])
    sin_d = din("sinT", [2048, 2048])
    iota_d = din("iota2", [2, NT])
    out_d = nc.dram_tensor("out", [LX, D], F32, kind="ExternalOutput").ap()

    def scr(name, shape, dt):
        kind = "ExternalOutput" if name in debug_outs else "Internal"
        return nc.dram_tensor(name, shape, dt, kind=kind).ap()

    xT = scr("xT", [D, NT], F32)
    pre = scr("pre", [D, NT], F32)
    hT = scr("hT", [D, NT], BF16)
    ufS = scr("ufS", [2048, NT], BF16)
    usS = scr("usS", [2048, NT], F32)
    gtS = scr("gtS", [8192, NT], BF16)
    fTS = scr("fTS", [2048, NT], BF16)
    m1S = scr("m1S", [D, NT], F32)
    ygS = scr("ygS", [2048, NT], BF16)
    mgS = scr("mgS", [D, NT], BF16)
    BsS = scr("BsS", [2, 128, 2 * 16 * 128], F32)
    modsS = scr("modsS", [DEPTH, 128, 192 * 2], F32)

    def fm(ap):
        return ap.rearrange("(k p) t -> p k t", p=128)

    es = C.es
    ps = [es.enter_context(nc.psum_tensor(f"ps{i}", [128, 512], F32)) for i in range(8)]
    _uid = [0]

    def sb(stack, name, shape, dt):
        _uid[0] += 1
        return stack.enter_context(nc.sbuf_tensor(f"{name}_u{_uid[0]}", shape, dt))

    ident = sb(es, "ident", [128, 128], F32)
    ones_b = sb(es, "ones_b", [128, 128], BF16)
    mods = [sb(es, f"mods{l}", [128, 192, 2], F32) for l in range(DEPTH)]
    lnp = [sb(es, f"lnp{l}", [128, 4, 32], F32) for l in range(DEPTH)]
    halfpi = sb(es, "halfpi", [128, 1], F32)
    epsT = sb(es, "epsT", [128, 1], F32)
    sgn = sb(es, "sgn", [128, 2], F32)
    magP = sb(es, "magP", [128, 1], F32)
    magN = sb(es, "magN", [128, 1], F32)
    vstage = sb(es, "vstage", [128, 128], F32)

    C.inc(SP.dma_start(out=ident[:], in_=ident_d[:, :]), "init", 16)
    C.inc(DVE.memset(ones_b[:], 1.0), "initv")
    C.inc(DVE.memset(halfpi[:], math.pi / 2), "initv")
    C.inc(DVE.memset(epsT[:], EPS), "initv")
    C.inc(DVE.memset(magP[:], MAGIC), "initv")
    C.inc(DVE.memset(magN[:], -MAGIC), "initv")
    C.inc(DVE.memset(sgn[:], -1.0), "initv")
    C.wait(DVE, "initv")
    C.inc(DVE.memset(sgn[0:64, 0:1], 1.0), "initv")
    C.barrier()

    def pe_mark(name):
        C.inc(PE.matmul(ps[7][0:2, 510:512], ones_b[:, 0:2], ones_b[:, 0:2], start=True, stop=True), name)
        return C.evt(name)

    def load_vec_fm(dst, src_ap, nch):
        done = 0
        while done < nch:
            n = min(128, nch - done)
            C.wait_evt(SP, C.free.get("vstage"))
            C.inc(SP.dma_start(out=vstage[0:n, :], in_=src_ap[done * 128:(done + n) * 128].rearrange("(c p) -> c p", p=128)), "vld", 16)
            C.wait(PE, "vld")
            C.bank_begin(7)
            C.bank_ready(PE.transpose(ps[7][:, 0:n], vstage[0:n, :], ident[0:n, 0:n]), 7)
            C.free["vstage"] = C.bank_evt(7)
            C.bank_wait(DVE, 7)
            C.bank_free(DVE.tensor_copy(out=dst[:, done:done + n], in_=ps[7][:, 0:n]), 7)
            done += n

    class Deferred:
        def __init__(self):
            self.q = []

        def push(self, fn):
            self.q.append(fn)

        def flush(self, keep=0):
            while len(self.q) > keep:
                self.q.pop(0)()

    class Stage:
        def __init__(self, stack, name, n, shape, dt):
            self.role = "SA" if dt == F32 else "SB"
            self.bufs = [sb(stack, f"{name}{i}", shape, dt) for i in range(n)]
            self.n = n
            self.c = 0
            self.base = [C.cnt(f"{self.role}{i}") for i in range(n)]

        def get(self, eng):
            i = self.c % self.n
            k = self.c // self.n
            self.c += 1
            C.wait(eng, f"{self.role}{i}", self.base[i] + 16 * k)
            return i, self.bufs[i]

        def stsem(self, i):
            return f"{self.role}{i}"

    class Loader:
        def __init__(self, stack, name, n, shape, dt):
            self.name = "LA" if dt == F32 else "LB"
            self.bufs = [sb(stack, f"{name}{i}", shape, dt) for i in range(n)]
            self.n = n
            self.c = 0
            self.pending = []
            self.freeev = [None] * n
            self.base = [C.cnt(f"{self.name}l{i}") for i in range(n)]

        def issue(self, dst_fn, src):
            i = self.c % self.n
            k = self.c // self.n
            self.c += 1
            C.wait_evt(SP, self.freeev[i])
            C.inc(SP.dma_start(out=dst_fn(self.bufs[i]), in_=src), f"{self.name}l{i}", 16)
            self.pending.append((i, k))

        def take(self, eng):
            i, k = self.pending.pop(0)
            C.wait(eng, f"{self.name}l{i}", self.base[i] + 16 * (k + 1))
            return i, self.bufs[i]

        def release(self, i, ev):
            self.freeev[i] = ev

    def gemm(name, act, KCn, wview, nblk, blkw, epi, wbufs, ntiles=TILES, banks=7):
        nsub = blkw // 128
        bsel = 0
        for j in range(nblk):
            i = j % len(wbufs)
            C.wait_evt(POOL, C.free.get((name, i)))
            for (dst, srcap) in wview(j, wbufs[i]):
                C.inc(POOL.dma_start(out=dst, in_=srcap), f"wl{i}", 16)
            C.wait(PE, f"wl{i}")
            for sub in range(nsub):
                fc = j * nsub + sub
                for ti, (t0, T, r) in enumerate(ntiles):
                    b = bsel % banks
                    bsel += 1
                    C.bank_begin(b)
                    for kc in range(KCn):
                        ins = PE.matmul(ps[b][:, :T], wbufs[i][:, kc, sub * 128:(sub + 1) * 128], act[:, kc, t0:t0 + T],
                                        start=(kc == 0), stop=(kc == KCn - 1))
                    C.bank_ready(ins, b)
                    C.free[(name, i)] = C.bank_evt(b)
                    epi(fc, ti, (t0, T, r), b)

    def phase_mods(l):
        with ExitStack() as ph:
            wb = [sb(ph, f"mwb{i}", [128, 32, 256], BF16) for i in range(2)]
            cs = sb(ph, "cs", [128, 2, 32], F32)
            csb = sb(ph, "csb", [128, 2, 32], BF16)
            brow = sb(ph, "brow", [1, 6 * D], F32)
            ones2 = sb(ph, "ones2", [1, 2], F32)
            for r in range(2):
                C.inc(SP.dma_start(out=cs[:, r, :], in_=cvec[r].rearrange("(p k) -> p k", k=32)), "mld", 16)
            C.inc(SP.dma_start(out=brow[:], in_=W["ada_b"][l:l + 1, :]), "mld", 16)
            C.inc(DVE.memset(ones2[:], 1.0), "mset")
            C.wait(ACT, "mld")
            C.inc(ACT.activation(out=csb[:], in_=cs[:], func=AF.Silu), "mset")
            C.wait(PE, "mset")
            C.wait(PE, "mld")
            wv = W["ada_w"][l].rearrange("(p k) n -> p k n", k=32)
            NB = 6 * D // 256
            for j in range(NB):
                i = j % 2
                C.wait_evt(POOL, C.free.get(("mw", i)))
                C.inc(POOL.dma_start(out=wb[i][:], in_=wv[:, :, j * 256:(j + 1) * 256]), f"wl{i}", 16)
                C.wait(PE, f"wl{i}")
                for sub in range(2):
                    nch = j * 2 + sub
                    b = nch % 4
                    C.bank_begin(b)
                    for kc in range(32):
                        PE.matmul(ps[b][:, 0:2], wb[i][:, kc, sub * 128:(sub + 1) * 128], csb[:, :, kc], start=(kc == 0), stop=False)
                    ins = PE.matmul(ps[b][:, 0:2], brow[0:1, nch * 128:(nch + 1) * 128], ones2[0:1, :], start=False, stop=True)
                    C.bank_ready(ins, b)
                    C.free[("mw", i)] = C.bank_evt(b)
                    C.bank_wait(DVE, b)
                    C.bank_free(DVE.tensor_copy(out=mods[l][:, nch, :], in_=ps[b][:, 0:2]), b)
            DVE.tensor_scalar(out=mods[l][:, 32:64, :], in0=mods[l][:, 32:64, :], scalar1=1.0, scalar2=None, op0=ALU.add)
            C.inc(DVE.tensor_scalar(out=mods[l][:, 128:160, :], in0=mods[l][:, 128:160, :], scalar1=1.0, scalar2=None, op0=ALU.add), "mset")
            for i, nm in enumerate(["ln1_g", "ln1_b", "ln2_g", "ln2_b"]):
                load_vec_fm(lnp[l][:, i, :], W[nm][l], 32)
            if "modsS" in debug_outs:
                C.wait(SP, "mset")
                C.inc(SP.dma_start(out=modsS[l], in_=mods[l][:].rearrange("p a b -> p (a b)")), "mld", 16)
            C.barrier()

    def phase_t0():
        with ExitStack() as ph:
            xin = [sb(ph, f"xin{i}", [128, D], F32) for i in range(2)]
            xst = [sb(ph, f"xst{i}", [128, 32, 128], F32) for i in range(2)]
            xTv = fm(xT)
            for it in range(18):
                i = it % 2
                k = it // 2
                src = xb[it * 128:(it + 1) * 128, :] if it < 16 else ctxb[(it - 16) * 128:(it - 15) * 128, :]
                C.wait_evt(SP, C.free.get(("xin", i)))
                C.inc(SP.dma_start(out=xin[i][:], in_=src), f"xinl{i}", 16)
                C.wait(PE, f"xinl{i}")
                C.wait(DVE, f"xsts{i}", 16 * k)
                C.wait(ACT, f"xsts{i}", 16 * k)
                evs = []
                for q in range(8):
                    b = q % 7
                    C.bank_begin(b)
                    for rr in range(4):
                        kc = q * 4 + rr
                        ins = PE.transpose(ps[b][:, rr * 128:(rr + 1) * 128], xin[i][:, kc * 128:(kc + 1) * 128], ident[:])
                    C.bank_ready(ins, b)
                    C.free[("xin", i)] = C.bank_evt(b)
                    eng = DVE if q % 2 == 0 else ACT
                    C.bank_wait(eng, b)
                    o = xst[i][:, q * 4:(q + 1) * 4, :]
                    iv = ps[b][:].rearrange("p (r t) -> p r t", t=128)
                    if eng is DVE:
                        ins = DVE.tensor_copy(out=o, in_=iv)
                    else:
                        ins = ACT.activation(out=o, in_=iv, func=AF.Copy)
                    C.bank_free(ins, b)
                    evs.append(C.evt(f"bf{b}"))
                C.wait_evt(SP, evs)
                C.inc(SP.dma_start(out=xTv[:, :, it * 128:(it + 1) * 128], in_=xst[i][:]), f"xsts{i}", 16)
            C.barrier()

    def ln_pass(l, src, kind):
        with ExitStack() as ph:
            yb = sb(ph, "ln_y", [128, 32, 512], F32)
            sqb = sb(ph, "ln_sq", [128, 32, 512], BF16)
            ybb = sb(ph, "ln_yb", [128, 32, 512], BF16)
            hob = sb(ph, "ln_ho", [128, 32, 512], BF16) if kind != "final" else None
            mean_t = sb(ph, "ln_mean", [128, 512], F32)
            rstd_t = sb(ph, "ln_rstd", [128, 512], F32)
            tmp_t = sb(ph, "ln_tmp", [128, 512], F32)
            ost = [sb(ph, f"ln_ost{i}", [128, 1024], F32) for i in range(2)] if kind == "final" else None
            srcv = fm(src)
            xTv = fm(xT)
            hTv = fm(hT)
            reads = []

            def stats_and_norm(T):
                b1, b2 = 0, 1
                C.bank_begin(b1)
                C.bank_begin(b2)
                for kc in range(32):
                    C.inc(ACT.activation(out=sqb[:, kc, :T], in_=yb[:, kc, :T], func=AF.Square), "lnsq")
                    C.inc(POOL.tensor_copy(out=ybb[:, kc, :T], in_=yb[:, kc, :T]), "lncp")
                    C.wait(PE, "lnsq")
                    C.wait(PE, "lncp")
                    i1 = PE.matmul(ps[b1][:, :T], ones_b[:], ybb[:, kc, :T], start=(kc == 0), stop=(kc == 31))
                    i2 = PE.matmul(ps[b2][:, :T], ones_b[:], sqb[:, kc, :T], start=(kc == 0), stop=(kc == 31))
                C.bank_ready(i1, b1)
                C.bank_ready(i2, b2)
                C.bank_wait(DVE, b1)
                C.bank_wait(DVE, b2)
                C.bank_free(DVE.tensor_scalar(out=mean_t[:, :T], in0=ps[b1][:, :T], scalar1=1.0 / D, scalar2=None, op0=ALU.mult), b1)
                DVE.tensor_tensor(out=tmp_t[:, :T], in0=mean_t[:, :T], in1=mean_t[:, :T], op=ALU.mult)
                ins = DVE.scalar_tensor_tensor(out=tmp_t[:, :T], in0=ps[b2][:, :T], scalar=1.0 / D, in1=tmp_t[:, :T], op0=ALU.mult, op1=ALU.subtract)
                C.bank_free(ins, b2)
                C.wait(ACT, f"bf{b2}")
                C.inc(ACT.activation(out=tmp_t[:, :T], in_=tmp_t[:, :T], func=AF.Sqrt, bias=epsT[:, 0:1], scale=1.0), "lnsd")
                C.wait(DVE, "lnsd")
                DVE.reciprocal(out=rstd_t[:, :T], in_=tmp_t[:, :T])
                for kc in range(32):
                    DVE.tensor_tensor(out=yb[:, kc, :T], in0=yb[:, kc, :T], in1=mean_t[:, :T], op=ALU.subtract)
                    C.inc(DVE.tensor_tensor(out=yb[:, kc, :T], in0=yb[:, kc, :T], in1=rstd_t[:, :T], op=ALU.mult), "lnn")

            for (t0, T, r) in TILES:
                if kind == "final" and r == 1:
                    continue
                C.wait_evt(SP, reads)
                reads = []
                C.inc(SP.dma_start(out=yb[:, :, :T], in_=srcv[:, :, t0:t0 + T]), "lnyl", 16)
                for e in (ACT, POOL, DVE):
                    C.wait(e, "lnyl")
                nn0 = C.cnt("lnn")
                stats_and_norm(T)
                if kind == "mod1":
                    steps = [("mod", l, 0)]
                elif kind == "post1":
                    steps = [("aff", l, 0), ("mod", l, 3)]
                elif kind == "post2":
                    steps = [("aff", l, 2)] + ([("mod", l + 1, 0)] if l + 1 < DEPTH else [])
                else:
                    steps = [("aff", l, 2)]
                for si, (st, ll, mi) in enumerate(steps):
                    if st == "aff":
                        for kc in range(32):
                            C.wait(ACT, "lnn", nn0 + kc + 1)
                            ins = ACT.activation(out=yb[:, kc, :T], in_=yb[:, kc, :T], func=AF.Identity,
                                                 bias=lnp[ll][:, mi + 1, kc:kc + 1], scale=lnp[ll][:, mi, kc:kc + 1])
                        C.inc(ins, "lnaf")
                        if kind == "final":
                            C.wait(PE, "lnaf")
                            oc = 0
                            for sub in range(T // 128):
                                for half in range(4):
                                    kq = C.use("ost")
                                    oi = kq % 2
                                    C.wait(DVE, f"osts{oi}", 16 * (kq // 2))
                                    for q in range(2):
                                        b = 2 + (half * 2 + q) % 4
                                        C.bank_begin(b)
                                        for rr in range(4):
                                            kc = half * 8 + q * 4 + rr
                                            ins = PE.transpose(ps[b][:, rr * 128:(rr + 1) * 128], yb[:, kc, sub * 128:(sub + 1) * 128], ident[:])
                                        C.bank_ready(ins, b)
                                        pe_last = C.bank_evt(b)
                                        C.bank_wait(DVE, b)
                                        ins = DVE.tensor_copy(out=ost[oi][:, q * 512:(q + 1) * 512], in_=ps[b][:])
                                        C.bank_free(ins, b)
                                    C.wait(SP, f"bf{b}")
                                    C.inc(SP.dma_start(out=out_d[t0 + sub * 128:t0 + (sub + 1) * 128, half * 1024:(half + 1) * 1024], in_=ost[oi][:]), f"osts{oi}", 16)
                            reads.append(pe_last)
                        else:
                            C.wait(SP, "lnaf")
                            C.inc(SP.dma_start(out=xTv[:, :, t0:t0 + T], in_=yb[:, :, :T]), "lnxs", 16)
                            reads.append(C.evt("lnxs"))
                        if si + 1 < len(steps):
                            C.wait(DVE, "lnxs")
                            C.wait(POOL, "lnaf")
                            nn0 = C.cnt("lnn")
                            stats_and_norm(T)
                    else:
                        kk = C.use("lnho")
                        C.wait(ACT, "lnhs", 16 * kk)
                        for kc in range(32):
                            C.wait(ACT, "lnn", nn0 + kc + 1)
                            ins = ACT.activation(out=hob[:, kc, :T], in_=yb[:, kc, :T], func=AF.Identity,
                                                 bias=mods[ll][:, mi * 32 + kc, r:r + 1], scale=mods[ll][:, (mi + 1) * 32 + kc, r:r + 1])
                        C.inc(ins, "lnmo")
                        C.wait(SP, "lnmo")
                        C.inc(SP.dma_start(out=hTv[:, :, t0:t0 + T], in_=hob[:, :, :T]), "lnhs", 16)
                        reads.append(C.evt("lnmo"))
                reads += [C.evt("lnsq"), C.evt("lncp"), C.evt("lnn"), C.evt("lnaf")]
            C.barrier()

    def phase_win(l):
        with ExitStack() as ph:
            act = sb(ph, "win_act", [128, 32, NT], BF16)
            wb = [sb(ph, f"win_wb{i}", [128, 32, 256], BF16) for i in range(2)]
            sf = Stage(ph, "winsf", 3, [128, 512], F32)
            sbf = Stage(ph, "winsb", 3, [128, 512], BF16)
            hTv = fm(hT)
            for q in range(4):
                C.inc(SP.dma_start(out=act[:, q * 8:(q + 1) * 8, :], in_=hTv[:, q * 8:(q + 1) * 8, :]), "actl", 16)
            C.wait(PE, "actl")
            wv = W["w_in"][l].rearrange("(k p) n -> p k n", p=128)
            ufv, usv, gtv = fm(ufS), fm(usS), fm(gtS)

            def wview(j, buf):
                return [(buf[:], wv[:, :, j * 256:(j + 1) * 256])]

            def epi(fc, ti, tl, b):
                t0, T, r = tl
                C.bank_wait(ACT, b)
                if fc < 16:
                    stg, fn, dst = sbf, AF.Copy, ufv[:, fc, t0:t0 + T]
                elif fc < 32:
                    stg, fn, dst = sf, AF.Copy, usv[:, fc - 16, t0:t0 + T]
                else:
                    stg, fn, dst = sbf, AF.Sigmoid, gtv[:, fc - 32, t0:t0 + T]
                i, st = stg.get(ACT)
                ins = ACT.activation(out=st[:, :T], in_=ps[b][:, :T], func=fn)
                C.bank_free(ins, b)
                C.wait(ACT, f"bf{b}")
                C.inc(ACT.dma_start(out=dst, in_=st[:, :T]), stg.stsem(i), 16)

            gemm("win", act, 32, wview, IN_COLS // 256, 256, epi, wbufs=wb)
            C.barrier()

    def phase_fourier_a(l):
        with ExitStack() as ph:
            CT = sb(ph, "fa_CT", [128, 16, 2048], BF16)
            ST = sb(ph, "fa_ST", [128, 16, 2048], BF16)
            UC = sb(ph, "fa_UC", [128, 16, 512], BF16)
            US = sb(ph, "fa_US", [128, 16, 512], BF16)
            uft = [sb(ph, f"fa_uf{i}", [128, 4, 2048], BF16) for i in range(2)]
            sbf = Stage(ph, "fasb", 3, [128, 512], BF16)
            cv = cos_d.rearrange("(c p) l -> p c l", p=128)
            sv = sin_d.rearrange("(c p) l -> p c l", p=128)
            for q in range(4):
                C.inc(POOL.dma_start(out=CT[:, q * 4:(q + 1) * 4, :], in_=cv[:, q * 4:(q + 1) * 4, :]), "actl", 16)
                C.inc(POOL.dma_start(out=ST[:, q * 4:(q + 1) * 4, :], in_=sv[:, q * 4:(q + 1) * 4, :]), "actl", 16)
            C.wait(PE, "actl")
            ufv = fm(ufS)
            fTv = fm(fTS)
            it = 0
            s2_done = None
            for (s0, L) in ((0, LX), (LX, LC)):
                ntc = L // 128
                step = 2048 // L
                for g in range(4):
                    i = it % 2
                    it += 1
                    C.wait_evt(SP, C.free.get(("fauf", i)))
                    C.inc(SP.dma_start(out=uft[i][:, :, :L], in_=ufv[:, g * 4:(g + 1) * 4, s0:s0 + L]), f"faul{i}", 16)
                    C.wait(PE, f"faul{i}")
                    C.wait_evt(DVE, s2_done)
                    C.wait_evt(ACT, s2_done)
                    evs = []
                    for tc in range(ntc):
                        for which, (TAB, DST) in enumerate(((CT, UC), (ST, US))):
                            b = (tc * 2 + which) % 6
                            C.bank_begin(b)
                            for chc in range(4):
                                ins = PE.matmul(ps[b][:, :], uft[i][:, chc, tc * 128:(tc + 1) * 128], TAB[:, chc, 0:2048:4],
                                                start=(chc == 0), stop=(chc == 3))
                            C.bank_ready(ins, b)
                            C.free[("fauf", i)] = C.bank_evt(b)
                            if which == 0:
                                C.bank_wait(DVE, b)
                                ins = DVE.tensor_copy(out=UC[:, tc, :], in_=ps[b][:, :])
                            else:
                                C.bank_wait(ACT, b)
                                ins = ACT.activation(out=US[:, tc, :], in_=ps[b][:, :], func=AF.Copy, scale=-1.0)
                            C.bank_free(ins, b)
                    for b in range(6):
                        C.wait(PE, f"bf{b}")
                    sc = 1.0 / math.sqrt(L * 512.0)
                    nlt = max(1, L // 512)
                    TL = min(L, 512)
                    for kq in range(4):
                        for lt in range(nlt):
                            b = (kq * nlt + lt) % 6
                            C.bank_begin(b)
                            for tc in range(ntc):
                                if step == 1:
                                    rc = CT[:, tc, lt * 512:(lt + 1) * 512]
                                    rs = ST[:, tc, lt * 512:(lt + 1) * 512]
                                else:
                                    rc = CT[:, tc, 0:2048:step]
                                    rs = ST[:, tc, 0:2048:step]
                                PE.matmul(ps[b][:, :TL], UC[:, tc, kq * 128:(kq + 1) * 128], rc, start=(tc == 0), stop=False)
                                ins = PE.matmul(ps[b][:, :TL], US[:, tc, kq * 128:(kq + 1) * 128], rs, start=False, stop=(tc == ntc - 1))
                            C.bank_ready(ins, b)
                            s2_done = C.bank_evt(b)
                            C.bank_wait(ACT, b)
                            si, st = sbf.get(ACT)
                            ins = ACT.activation(out=st[:, :TL], in_=ps[b][:, :TL], func=AF.Copy, scale=sc)
                            C.bank_free(ins, b)
                            C.wait(ACT, f"bf{b}")
                            C.inc(ACT.dma_start(out=fTv[:, g * 4 + kq, s0 + lt * 512:s0 + lt * 512 + TL], in_=st[:, :TL]), sbf.stsem(si), 16)
            C.barrier()

    def phase_fourier_b(l):
        with ExitStack() as ph:
            act = sb(ph, "fb_act", [128, 16, NT], BF16)
            wb = [sb(ph, f"fb_wb{i}", [128, 16, 512], BF16) for i in range(2)]
            sf = Stage(ph, "fbsf", 3, [128, 512], F32)
            gl = Loader(ph, "fbg", 3, [128, 512], BF16)
            fTv = fm(fTS)
            for q in range(2):
                C.inc(SP.dma_start(out=act[:, q * 8:(q + 1) * 8, :], in_=fTv[:, q * 8:(q + 1) * 8, :]), "actl", 16)
            C.wait(PE, "actl")
            wv = W["fourier_w"][l].rearrange("(k p) n -> p k n", p=128)
            gtv, m1v = fm(gtS), fm(m1S)

            def wview(j, buf):
                return [(buf[:], wv[:, :, j * 512:(j + 1) * 512])]

            def epi(fc, ti, tl, b):
                t0, T, r = tl
                gl.issue(lambda bf: bf[:, :T], gtv[:, fc, t0:t0 + T])
                gi, gb = gl.take(DVE)
                C.bank_wait(DVE, b)
                si, st = sf.get(DVE)
                ins = DVE.tensor_tensor(out=st[:, :T], in0=ps[b][:, :T], in1=gb[:, :T], op=ALU.mult)
                C.bank_free(ins, b)
                gl.release(gi, C.evt(f"bf{b}"))
                C.wait(ACT, f"bf{b}")
                C.inc(ACT.dma_start(out=m1v[:, fc, t0:t0 + T], in_=st[:, :T]), sf.stsem(si), 16)

            gemm("fb", act, 16, wview, D // 512, 512, epi, wbufs=wb)
            C.barrier()

    def phase_ssm_prep(l, RT, TH, CM, Dv):
        with ExitStack() as ph:
            are = sb(ph, "sp_are", [128, 64], F32)
            aim = sb(ph, "sp_aim", [128, 64], F32)
            ldt = sb(ph, "sp_ldt", [128, 1], F32)
            bre = sb(ph, "sp_bre", [128, 1024], F32)
            bim = sb(ph, "sp_bim", [128, 1024], F32)
            t = [sb(ph, f"sp_t{i}", [128, 64], F32) for i in range(14)]
            RR = sb(ph, "sp_RR", [128, 128], F32)
            TT = sb(ph, "sp_TT", [128, 128], F32)
            BB = sb(ph, "sp_BB", [128, 2, 16, 128], F32)
            big = [sb(ph, f"sp_big{i}", [128, 16, 64], F32) for i in range(4)]
            CC = [sb(ph, f"sp_CC{i}", [128, 128], F32) for i in range(2)]
            load_vec_fm(Dv[:, :], W["ssm_d"][l], 16)
            for d in range(2):
                C.wait(SP, "spv")
                C.wait(SP, "spa")
                C.inc(SP.dma_start(out=are[:], in_=W["ssm_a_re"][l, d]), "spl", 16)
                C.inc(SP.dma_start(out=aim[:], in_=W["ssm_a_im"][l, d]), "spl", 16)
                C.inc(SP.dma_start(out=ldt[:], in_=W["ssm_log_dt"][l, d].rearrange("(g o) -> g o", o=1)), "spl", 16)
                C.inc(SP.dma_start(out=bre[:], in_=W["ssm_b_re"][l, d].rearrange("g p c -> g (p c)")), "spl", 16)
                C.inc(SP.dma_start(out=bim[:], in_=W["ssm_b_im"][l, d].rearrange("g p c -> g (p c)")), "spl", 16)
                C.wait(ACT, "spl")
                C.wait(DVE, "spl")
                dt_, lr, thr, rho, fr, sn, cs_, abr, abi, x0, nr, den, cfr, cfi = t
                C.inc(ACT.activation(out=dt_[:, 0:1], in_=ldt[:], func=AF.Exp), "spa")
                C.wait(DVE, "spa")
                DVE.tensor_scalar(out=lr[:], in0=are[:], scalar1=dt_[:, 0:1], scalar2=None, op0=ALU.mult)
                DVE.tensor_scalar(out=thr[:], in0=aim[:], scalar1=dt_[:, 0:1], scalar2=1.0 / (2 * math.pi), op0=ALU.mult, op1=ALU.mult)
                DVE.tensor_scalar(out=x0[:], in0=thr[:], scalar1=MAGIC, scalar2=None, op0=ALU.add)
                DVE.tensor_scalar(out=x0[:], in0=x0[:], scalar1=-MAGIC, scalar2=None, op0=ALU.add)
                C.inc(DVE.tensor_tensor(out=fr[:], in0=thr[:], in1=x0[:], op=ALU.subtract), "spv")
                C.wait(ACT, "spv")
                ACT.activation(out=rho[:], in_=lr[:], func=AF.Exp)
                ACT.activation(out=sn[:], in_=fr[:], func=AF.Sin, scale=TWO_PI_S)
                ACT.activation(out=x0[:], in_=fr[:], func=AF.Abs)
                C.inc(ACT.activation(out=cs_[:], in_=x0[:], func=AF.Sin, scale=-TWO_PI_S, bias=halfpi[:, 0:1]), "spa")
                C.wait(DVE, "spa")
                DVE.tensor_tensor(out=abr[:], in0=rho[:], in1=cs_[:], op=ALU.mult)
                DVE.tensor_tensor(out=abi[:], in0=rho[:], in1=sn[:], op=ALU.mult)
                for h in range(2):
                    DVE.tensor_copy(out=RR[:, h * 64:(h + 1) * 64], in_=rho[:])
                    DVE.tensor_copy(out=TT[:, h * 64:(h + 1) * 64], in_=thr[:])
                u1 = x0
                DVE.tensor_scalar(out=nr[:], in0=abr[:], scalar1=-1.0, scalar2=None, op0=ALU.add)
                DVE.tensor_tensor(out=den[:], in0=are[:], in1=are[:], op=ALU.mult)
                DVE.tensor_tensor(out=u1[:], in0=aim[:], in1=aim[:], op=ALU.mult)
                DVE.tensor_tensor(out=den[:], in0=den[:], in1=u1[:], op=ALU.add)
                DVE.reciprocal(out=den[:], in_=den[:])
                DVE.tensor_tensor(out=cfr[:], in0=nr[:], in1=are[:], op=ALU.mult)
                DVE.tensor_tensor(out=u1[:], in0=abi[:], in1=aim[:], op=ALU.mult)
                DVE.tensor_tensor(out=cfr[:], in0=cfr[:], in1=u1[:], op=ALU.add)
                DVE.tensor_tensor(out=cfr[:], in0=cfr[:], in1=den[:], op=ALU.mult)
                DVE.tensor_tensor(out=cfi[:], in0=abi[:], in1=are[:], op=ALU.mult)
                DVE.tensor_tensor(out=u1[:], in0=nr[:], in1=aim[:], op=ALU.mult)
                DVE.tensor_tensor(out=cfi[:], in0=cfi[:], in1=u1[:], op=ALU.subtract)
                DVE.tensor_tensor(out=cfi[:], in0=cfi[:], in1=den[:], op=ALU.mult)
                brv = bre[:].rearrange("g (p c) -> g c p", c=16)
                biv = bim[:].rearrange("g (p c) -> g c p", c=16)
                cfrb = cfr[:].rearrange("g (o p) -> g o p", o=1).to_broadcast([128, 16, 64])
                cfib = cfi[:].rearrange("g (o p) -> g o p", o=1).to_broadcast([128, 16, 64])
                C.wait(DVE, "spbs")
                DVE.tensor_tensor(out=big[0][:], in0=brv, in1=cfrb, op=ALU.mult)
                DVE.tensor_tensor(out=big[1][:], in0=biv, in1=cfib, op=ALU.mult)
                DVE.tensor_tensor(out=big[2][:], in0=biv, in1=cfrb, op=ALU.mult)
                DVE.tensor_tensor(out=big[3][:], in0=brv, in1=cfib, op=ALU.mult)
                DVE.tensor_tensor(out=BB[:, 0, :, 0:64], in0=big[0][:], in1=big[1][:], op=ALU.subtract)
                DVE.tensor_tensor(out=BB[:, 0, :, 64:128], in0=big[2][:], in1=big[3][:], op=ALU.add)
                DVE.tensor_copy(out=BB[:, 1, :, 0:64], in_=BB[:, 0, :, 64:128])
                C.inc(DVE.tensor_scalar(out=BB[:, 1, :, 64:128], in0=BB[:, 0, :, 0:64], scalar1=-1.0, scalar2=None, op0=ALU.mult), "spv")
                C.wait(SP, "spv")
                C.inc(SP.dma_start(out=BsS[d], in_=BB[:].rearrange("g v c q -> g (v c q)")), "spbs", 16)
                C.wait(PE, "spv")
                for src_t, dst_t in ((RR, RT), (TT, TH)):
                    C.bank_begin(6)
                    C.bank_ready(PE.transpose(ps[6][:, 0:128], src_t[:], ident[:]), 6)
                    C.bank_wait(DVE, 6)
                    C.bank_free(DVE.tensor_copy(out=dst_t[:, d, :], in_=ps[6][:, 0:128]), 6)
                crv = W["ssm_c_re"][l, d].rearrange("(cb g) co p -> cb (g co) p", g=8)
                civ = W["ssm_c_im"][l, d].rearrange("(cb g) co p -> cb (g co) p", g=8)
                for cb in range(16):
                    for v in range(2):
                        C.wait_evt(SP, C.free.get(("spcc", v)))
                        a, bsrc = (crv, civ) if v == 0 else (civ, crv)
                        C.inc(SP.dma_start(out=CC[v][:, 0:64], in_=a[cb]), f"spccl{v}", 16)
                        C.inc(SP.dma_start(out=CC[v][:, 64:128], in_=bsrc[cb]), f"spccl{v}", 16)
                        C.wait(PE, f"spccl{v}")
                        b = 4 + v
                        C.bank_begin(b)
                        C.bank_ready(PE.transpose(ps[b][:, 0:128], CC[v][:], ident[:]), b)
                        C.free[("spcc", v)] = C.bank_evt(b)
                        C.bank_wait(DVE, b)
                        ins = DVE.tensor_scalar(out=CM[:, d, cb * 8:(cb + 1) * 8, v, :],
                                                in0=ps[b][:, 0:128].rearrange("q (g co) -> q g co", co=16),
                                                scalar1=sgn[:, v:v + 1], scalar2=None, op0=ALU.mult)
                        C.bank_free(ins, b)
            C.barrier()

    def phase_ssm(l):
        with ExitStack() as ph:
            RT = sb(ph, "ss_RT", [128, 2, 128], F32)
            TH = sb(ph, "ss_TH", [128, 2, 128], F32)
            CM = sb(ph, "ss_CM", [128, 2, 128, 2, 16], BF16)
            Dv = sb(ph, "ss_Dv", [128, 16], F32)
            phase_ssm_prep(l, RT, TH, CM, Dv)
            iota = sb(ph, "ss_iota", [128, 2, NT], F32)
            ust = [sb(ph, f"ss_ust{i}", [128, NT], BF16) for i in range(2)]
            usf = [sb(ph, f"ss_usf{i}", [128, NT], F32) for i in range(2)]
            ZB = [sb(ph, f"ss_ZB{i}", [128, 8, 2, 128], BF16) for i in range(2)]
            ZC = [sb(ph, f"ss_ZC{i}", [128, 8, 2, 128], BF16) for i in range(2)]
            snT = [sb(ph, f"ss_sn{i}", [128, NT], F32) for i in range(2)]
            csT = [sb(ph, f"ss_cs{i}", [128, NT], F32) for i in range(2)]
            t1 = sb(ph, "ss_t1", [128, NT], F32)
            xxT = sb(ph, "ss_xx", [128, NT], F32)
            frT = sb(ph, "ss_fr", [128, NT], F32)
            abT = sb(ph, "ss_ab", [128, NT], F32)
            wT = sb(ph, "ss_w", [128, NT], F32)
            tmpT = sb(ph, "ss_tmp", [128, 512], F32)
            gT = sb(ph, "ss_g", [128, NT], F32)
            A1 = [sb(ph, f"ss_A1{i}", [128, NT], BF16) for i in range(2)]
            A2 = [sb(ph, f"ss_A2{i}", [128, NT], BF16) for i in range(2)]
            ytmp = sb(ph, "ss_yt", [128, 512], F32)
            ygs = [sb(ph, f"ss_yg{i}", [128, NT], BF16) for i in range(2)]
            for r in range(2):
                C.inc(SP.dma_start(out=iota[:, r, :], in_=iota_d[r:r + 1, :].partition_broadcast(128)), "ssi", 16)
            for i in range(2):
                C.inc(POOL.memset(ZB[i][:], 0.0), "ssz")
                C.inc(POOL.memset(ZC[i][:], 0.0), "ssz")
            C.barrier()
            usv = fm(usS)
            ygv = fm(ygS)
            BsV = BsS.rearrange("d g (v c q) -> d g c v q", v=2, c=16)
            PIECES = [(0, 512), (512, 512), (1024, 512), (1536, 512), (2048, 256)]
            its = [(blk, d, j) for blk in range(16) for d in range(2) for j in range(8)]

            def emit_tables(n):
                blk, d, j = its[n]
                g = blk * 8 + j
                ti = n % 2
                C.wait_evt(POOL, C.free.get("frT"))
                io = iota[:, d, :]
                C.wait_evt(ACT, C.free.get("xxT"))
                ACT.activation(out=xxT[:], in_=io, func=AF.Copy, scale=TH[:, d, g:g + 1])
                ACT.activation(out=t1[:], in_=xxT[:], func=AF.Identity, bias=magP[:, 0:1], scale=1.0)
                C.inc(ACT.activation(out=t1[:], in_=t1[:], func=AF.Identity, bias=magN[:, 0:1], scale=1.0), "sstq")
                C.wait(POOL, "sstq")
                C.inc(POOL.tensor_tensor(out=frT[:], in0=xxT[:], in1=t1[:], op=ALU.subtract), "sstp")
                C.free["xxT"] = C.evt("sstp")
                C.wait(ACT, "sstp")
                C.wait_evt(ACT, C.free.get(("tab", ti)))
                ACT.activation(out=snT[ti][:], in_=frT[:], func=AF.Sin, scale=TWO_PI_S)
                ACT.activation(out=abT[:], in_=frT[:], func=AF.Abs)
                C.inc(ACT.activation(out=csT[ti][:], in_=abT[:], func=AF.Sin, scale=-TWO_PI_S, bias=halfpi[:, 0:1]), "ssta")
                C.free["frT"] = C.evt("ssta")
                return C.evt("ssta")

            tab_ready = {0: emit_tables(0)}
            pending_ro = [None]
            for n, (blk, d, j) in enumerate(its):
                g = blk * 8 + j
                ui = blk % 2
                zi = (blk * 2 + d) % 2
                ti = n % 2
                ai = n % 2
                first = (d == 0 and j == 0)
                last = (d == 1 and j == 7)
                if first:
                    C.wait_evt(POOL, C.free.get(("ust", ui)))
                    C.inc(POOL.dma_start(out=ust[ui][:], in_=usv[:, blk, :]), f"ssul{ui}", 16)
                    C.wait_evt(SP, C.free.get(("usf", ui)))
                    C.inc(SP.dma_start(out=usf[ui][:], in_=usv[:, blk, :]), f"ssvl{ui}", 16)
                if j == 0:
                    C.wait_evt(POOL, C.free.get(("Z", zi)))
                    for jj in range(8):
                        C.inc(POOL.dma_start(out=ZB[zi][16 * jj:16 * (jj + 1), jj, :, :], in_=BsV[d, blk * 8 + jj]), f"sszl{zi}", 16)
                    C.wait_evt(ACT, C.free.get(("Z", zi)))
                    for jj in range(8):
                        ins = ACT.activation(out=ZC[zi][:, jj, :, 16 * jj:16 * (jj + 1)], in_=CM[:, d, blk * 8 + jj, :, :], func=AF.Copy)
                    C.inc(ins, f"sszc{zi}")
                if n + 1 < len(its):
                    tab_ready[n + 1] = emit_tables(n + 1)
                C.wait_evt(DVE, tab_ready.pop(n))
                C.wait(PE, f"ssul{ui}")
                C.wait(PE, f"sszl{zi}")
                for (p0, PL) in PIECES:
                    C.bank_begin(5)
                    C.bank_ready(PE.matmul(ps[5][:, :PL], ZB[zi][:, j, 0, :], ust[ui][:, p0:p0 + PL], start=True, stop=True), 5)
                    C.bank_begin(6)
                    C.bank_ready(PE.matmul(ps[6][:, :PL], ZB[zi][:, j, 1, :], ust[ui][:, p0:p0 + PL], start=True, stop=True), 6)
                    C.bank_wait(DVE, 5)
                    C.bank_wait(DVE, 6)
                    C.bank_free(DVE.tensor_tensor(out=wT[:, p0:p0 + PL], in0=ps[5][:, :PL], in1=csT[ti][:, p0:p0 + PL], op=ALU.mult), 5)
                    C.bank_free(DVE.tensor_tensor(out=tmpT[:, :PL], in0=ps[6][:, :PL], in1=snT[ti][:, p0:p0 + PL], op=ALU.mult), 6)
                    ins = DVE.tensor_tensor(out=wT[:, p0:p0 + PL], in0=wT[:, p0:p0 + PL], in1=tmpT[:, :PL], op=ALU.add)
                C.inc(ins, "ssw")
                if pending_ro[0] is not None:
                    pending_ro[0]()
                    pending_ro[0] = None
                C.wait(DVE, "ssw")
                C.wait_evt(DVE, C.free.get("gT"))
                rb = RT[:, d, g:g + 1]
                if d == 0:
                    C.inc(DVE.tensor_tensor_scan(out=gT[:, LX:NT], data0=rb.to_broadcast([128, LC]), data1=wT[:, LX:NT], initial=0.0, op0=ALU.mult, op1=ALU.add), "ssw")
                    C.wait(DVE, "ssw")
                    C.inc(DVE.tensor_tensor_scan(out=gT[:, 0:LX], data0=rb.to_broadcast([128, LX]), data1=wT[:, 0:LX], initial=gT[:, NT - 1:NT], op0=ALU.mult, op1=ALU.add), "ssw")
                else:
                    C.inc(DVE.tensor_tensor_scan(out=gT[:, LX:NT][:, ::-1], data0=rb.to_broadcast([128, LC]), data1=wT[:, LX:NT][:, ::-1], initial=0.0, op0=ALU.mult, op1=ALU.add), "ssw")
                    C.wait(DVE, "ssw")
                    C.inc(DVE.tensor_tensor_scan(out=gT[:, 0:LX][:, ::-1], data0=rb.to_broadcast([128, LX]), data1=wT[:, 0:LX][:, ::-1], initial=gT[:, LX:LX + 1], op0=ALU.mult, op1=ALU.add), "ssw")
                C.wait(POOL, "ssw")
                C.wait_evt(POOL, C.free.get(("A", ai)))
                POOL.tensor_tensor(out=A1[ai][:], in0=gT[:], in1=csT[ti][:], op=ALU.mult)
                C.inc(POOL.tensor_tensor(out=A2[ai][:], in0=gT[:], in1=snT[ti][:], op=ALU.mult), f"ssal{ai}")
                C.free[("tab", ti)] = C.evt(f"ssal{ai}")
                C.free["gT"] = C.evt(f"ssal{ai}")
                ev_al = C.evt(f"ssal{ai}")
                ev_zc = C.evt(f"sszc{zi}")

                def ro(blk=blk, d=d, j=j, ui=ui, zi=zi, ai=ai, first=first, last=last, ev_al=ev_al, ev_zc=ev_zc):
                    C.wait_evt(PE, ev_al)
                    C.wait_evt(PE, ev_zc)
                    for pi, (p0, PL) in enumerate(PIECES):
                        if first:
                            C.bank_begin(pi)
                        PE.matmul(ps[pi][:, :PL], ZC[zi][:, j, 0, :], A1[ai][:, p0:p0 + PL], start=first, stop=False)
                        ins = PE.matmul(ps[pi][:, :PL], ZC[zi][:, j, 1, :], A2[ai][:, p0:p0 + PL], start=False, stop=last)
                        if last:
                            C.bank_ready(ins, pi)
                    ev = pe_mark("ssam")
                    C.free[("A", ai)] = ev
                    if j == 7:
                        C.free[("Z", zi)] = ev
                    if last:
                        C.free[("ust", ui)] = ev
                        yi = blk % 2
                        C.wait(DVE, f"ssvl{ui}")
                        C.wait(ACT, f"ssygs{yi}", 16 * (blk // 2))
                        for pi, (p0, PL) in enumerate(PIECES):
                            C.bank_wait(DVE, pi)
                            C.wait(DVE, "ssge")
                            ins = DVE.scalar_tensor_tensor(out=ytmp[:, :PL], in0=usf[ui][:, p0:p0 + PL], scalar=Dv[:, blk:blk + 1], in1=ps[pi][:, :PL], op0=ALU.mult, op1=ALU.add)
                            C.bank_free(ins, pi)
                            C.wait(ACT, f"bf{pi}")
                            C.inc(ACT.activation(out=ygs[yi][:, p0:p0 + PL], in_=ytmp[:, :PL], func=AF.Gelu_apprx_tanh), "ssge")
                        C.free[("usf", ui)] = C.evt("ssge")
                        C.wait(ACT, "ssge")
                        C.inc(ACT.dma_start(out=ygv[:, blk, :], in_=ygs[yi][:]), f"ssygs{yi}", 16)
                pending_ro[0] = ro
            pending_ro[0]()
            C.barrier()

    def phase_glu(l):
        with ExitStack() as ph:
            act = sb(ph, "gl_act", [128, 16, NT], BF16)
            wb = [sb(ph, f"gl_wb{i}", [128, 16, 512], BF16) for i in range(2)]
            sbf = Stage(ph, "glsb", 3, [128, 512], BF16)
            gl = Loader(ph, "glg", 3, [128, 512], BF16)
            ml = Loader(ph, "glm", 3, [128, 512], F32)
            sg = [sb(ph, f"gl_sg{i}", [128, 512], F32) for i in range(2)]
            dq = Deferred()
            ygv = fm(ygS)
            for q in range(2):
                C.inc(SP.dma_start(out=act[:, q * 8:(q + 1) * 8, :], in_=ygv[:, q * 8:(q + 1) * 8, :]), "actl", 16)
            C.wait(PE, "actl")
            wv = W["glu_w"][l].rearrange("(k p) n -> p k n", p=128)
            gtv, m1v, mgv = fm(gtS), fm(m1S), fm(mgS)
            bsel = 0
            for j in range(D // 256):
                i = j % 2
                C.wait_evt(POOL, C.free.get(("glw", i)))
                C.inc(POOL.dma_start(out=wb[i][:, :, 0:256], in_=wv[:, :, j * 256:(j + 1) * 256]), f"wl{i}", 16)
                C.inc(POOL.dma_start(out=wb[i][:, :, 256:512], in_=wv[:, :, D + j * 256:D + (j + 1) * 256]), f"wl{i}", 16)
                C.wait(PE, f"wl{i}")
                for sub in range(2):
                    fc = j * 2 + sub
                    for ti, (t0, T, r) in enumerate(TILES):
                        bv = (bsel % 3) * 2
                        bg = bv + 1
                        si_ = bsel % 2
                        bsel += 1
                        C.bank_begin(bv)
                        for kc in range(16):
                            ins = PE.matmul(ps[bv][:, :T], wb[i][:, kc, sub * 128:(sub + 1) * 128], act[:, kc, t0:t0 + T], start=(kc == 0), stop=(kc == 15))
                        C.bank_ready(ins, bv)
                        C.bank_begin(bg)
                        for kc in range(16):
                            ins = PE.matmul(ps[bg][:, :T], wb[i][:, kc, 256 + sub * 128:256 + (sub + 1) * 128], act[:, kc, t0:t0 + T], start=(kc == 0), stop=(kc == 15))
                        C.bank_ready(ins, bg)
                        C.free[("glw", i)] = C.bank_evt(bg)
                        gl.issue(lambda bf: bf[:, :T], gtv[:, 32 + fc, t0:t0 + T])
                        ml.issue(lambda bf: bf[:, :T], m1v[:, fc, t0:t0 + T])
                        C.bank_wait(ACT, bg)
                        C.wait_evt(ACT, C.free.get(("glsg", si_)))
                        ins = ACT.activation(out=sg[si_][:, :T], in_=ps[bg][:, :T], func=AF.Sigmoid)
                        C.bank_free(ins, bg)
                        dq.flush()
                        C.wait(DVE, f"bf{bg}")
                        C.bank_wait(DVE, bv)
                        ins = DVE.tensor_tensor(out=sg[si_][:, :T], in0=ps[bv][:, :T], in1=sg[si_][:, :T], op=ALU.mult)
                        C.bank_free(ins, bv)
                        gi, gb = gl.take(DVE)
                        mi, mb = ml.take(DVE)
                        DVE.tensor_tensor(out=sg[si_][:, :T], in0=sg[si_][:, :T], in1=gb[:, :T], op=ALU.mult)
                        sti, st = sbf.get(DVE)
                        ins = DVE.tensor_tensor(out=st[:, :T], in0=sg[si_][:, :T], in1=mb[:, :T], op=ALU.add)
                        C.inc(ins, "ep")
                        ev = C.evt("ep")
                        gl.release(gi, ev)
                        ml.release(mi, ev)
                        C.free[("glsg", si_)] = ev

                        def store(ev=ev, dst=mgv[:, fc, t0:t0 + T], src=st[:, :T], nm=sbf.stsem(sti)):
                            C.wait_evt(ACT, ev)
                            C.inc(ACT.dma_start(out=dst, in_=src), nm, 16)
                        dq.push(store)
            dq.flush()
            C.barrier()

    def make_resid_epi(ph, l, gate_m, name):
        sf = Stage(ph, f"{name}sf", 2, [128, 512], F32)
        xl = Loader(ph, f"{name}x", 2, [128, 512], F32)
        tb = [sb(ph, f"{name}_t{i}", [128, 512], F32) for i in range(2)]
        xTv, prev = fm(xT), fm(pre)
        cnt = [0]
        dq = Deferred()

        def epi(fc, ti, tl, b):
            t0, T, r = tl
            n = cnt[0]
            cnt[0] += 1
            i2 = n % 2
            xl.issue(lambda bf: bf[:, :T], xTv[:, fc, t0:t0 + T])
            C.bank_wait(ACT, b)
            C.wait_evt(ACT, C.free.get((name, "tb", i2)))
            ins = ACT.activation(out=tb[i2][:, :T], in_=ps[b][:, :T], func=AF.Copy, scale=mods[l][:, gate_m * 32 + fc, r:r + 1])
            C.bank_free(ins, b)
            dq.flush()
            C.wait(DVE, f"bf{b}")
            xi, xbuf = xl.take(DVE)
            si, st = sf.get(DVE)
            ins = DVE.scalar_tensor_tensor(out=st[:, :T], in0=xbuf[:, :T], scalar=ALPHA, in1=tb[i2][:, :T], op0=ALU.mult, op1=ALU.add)
            C.inc(ins, "ep")
            ev = C.evt("ep")
            xl.release(xi, ev)
            C.free[(name, "tb", i2)] = ev

            def store(ev=ev, dst=prev[:, fc, t0:t0 + T], src=st[:, :T], nm=sf.stsem(si)):
                C.wait_evt(ACT, ev)
                C.inc(ACT.dma_start(out=dst, in_=src), nm, 16)
            dq.push(store)
        return epi, dq

    def phase_wout(l):
        with ExitStack() as ph:
            act = sb(ph, "wo_act", [128, 32, NT], BF16)
            wb = [sb(ph, f"wo_wb{i}", [128, 32, 256], BF16) for i in range(2)]
            mgv = fm(mgS)
            for q in range(4):
                C.inc(SP.dma_start(out=act[:, q * 8:(q + 1) * 8, :], in_=mgv[:, q * 8:(q + 1) * 8, :]), "actl", 16)
            C.wait(PE, "actl")
            wv = W["w_out"][l].rearrange("(k p) n -> p k n", p=128)
            epi, dq = make_resid_epi(ph, l, 2, "wo")

            def wview(j, buf):
                return [(buf[:], wv[:, :, j * 256:(j + 1) * 256])]
            gemm("wo", act, 32, wview, D // 256, 256, epi, wbufs=wb)
            dq.flush()
            C.barrier()

    def phase_ffn(l):
        with ExitStack() as ph:
            h2 = sb(ph, "ff_h2", [128, 32, 512], BF16)
            actT = sb(ph, "ff_act", [128, HB, 512], BF16)
            wu = [sb(ph, f"ff_wu{i}", [128, 32, 256], BF16) for i in range(2)]
            wvb = [sb(ph, f"ff_wv{i}", [128, 32, 256], BF16) for i in range(2)]
            cw = sb(ph, "ff_cw", [128, 3, HB], F32)
            cbias = sb(ph, "ff_cb", [128, HB], F32)
            cbuf = [sb(ph, f"ff_c{i}", [128, 512], F32) for i in range(2)]
            for kk in range(3):
                load_vec_fm(cw[:, kk, :], W["ffn_conv_w"][l, kk], HB)
            load_vec_fm(cbias[:, :], W["ffn_conv_b"][l], HB)
            C.barrier()
            hTv = fm(hT)
            w12 = W["ffn_w12"][l].rearrange("(k p) n -> p k n", p=128)
            w2 = W["ffn_w2"][l].rearrange("(k p) n -> p k n", p=128)
            epi2, dq = make_resid_epi(ph, l, 5, "ff")
            w2v = [wu[i][:].rearrange("p a b -> p (a b)")[:, 0:43 * 128].rearrange("p (a b) -> p a b", b=128) for i in range(2)]
            ec = 0
            for ti, (t0, T, r) in enumerate(TILES):
                C.wait_evt(SP, C.free.get("ffh2"))
                C.inc(SP.dma_start(out=h2[:, :, :T], in_=hTv[:, :, t0:t0 + T]), "actl", 16)
                C.wait(PE, "actl")
                rows = T // 64 if r == 0 else 1
                RL = T // rows
                for j in range(FFN // 256):
                    i = j % 2
                    C.wait_evt(POOL, C.free.get(("ffwu", i)))
                    C.inc(POOL.dma_start(out=wu[i][:], in_=w12[:, :, j * 256:(j + 1) * 256]), f"wl{i}", 16)
                    C.wait_evt(POOL, C.free.get(("ffwv", i)))
                    C.inc(POOL.dma_start(out=wvb[i][:], in_=w12[:, :, FFN + j * 256:FFN + (j + 1) * 256]), f"wl{i}", 16)
                    C.wait(PE, f"wl{i}")
                    for sub in range(2):
                        hb = j * 2 + sub
                        bu = (ec % 2) * 2
                        bv = bu + 1
                        ci = ec % 2
                        ec += 1
                        C.bank_begin(bu)
                        for kc in range(32):
                            ins = PE.matmul(ps[bu][:, :T], wu[i][:, kc, sub * 128:(sub + 1) * 128], h2[:, kc, :T], start=(kc == 0), stop=(kc == 31))
                        C.bank_ready(ins, bu)
                        C.free[("ffwu", i)] = C.bank_evt(bu)
                        C.bank_begin(bv)
                        for kc in range(32):
                            ins = PE.matmul(ps[bv][:, :T], wvb[i][:, kc, sub * 128:(sub + 1) * 128], h2[:, kc, :T], start=(kc == 0), stop=(kc == 31))
                        C.bank_ready(ins, bv)
                        C.free[("ffwv", i)] = C.bank_evt(bv)
                        C.free["ffh2"] = C.bank_evt(bv)
                        C.bank_wait(ACT, bu)
                        C.wait_evt(ACT, C.free.get(("ffg", ci)))
                        C.inc(ACT.activation(out=cbuf[ci][:, :T], in_=ps[bu][:, :T], func=AF.Identity,
                                             bias=cbias[:, hb:hb + 1], scale=cw[:, 1, hb:hb + 1]), "ffc1")
                        C.wait(DVE, "ffc1")
                        c3 = cbuf[ci][:, :T].rearrange("p (a w) -> p a w", w=RL)
                        u3 = ps[bu][:, :T].rearrange("p (a w) -> p a w", w=RL)
                        DVE.scalar_tensor_tensor(out=c3[:, :, 1:RL], in0=u3[:, :, 0:RL - 1], scalar=cw[:, 0, hb:hb + 1], in1=c3[:, :, 1:RL], op0=ALU.mult, op1=ALU.add)
                        ins = DVE.scalar_tensor_tensor(out=c3[:, :, 0:RL - 1], in0=u3[:, :, 1:RL], scalar=cw[:, 2, hb:hb + 1], in1=c3[:, :, 0:RL - 1], op0=ALU.mult, op1=ALU.add)
                        C.bank_free(ins, bu)
                        C.wait(ACT, f"bf{bu}")
                        C.inc(ACT.activation(out=cbuf[ci][:, :T], in_=cbuf[ci][:, :T], func=AF.Gelu_apprx_tanh), "ffc2")
                        C.wait(DVE, "ffc2")
                        C.bank_wait(DVE, bv)
                        C.wait_evt(DVE, C.free.get("ffact"))
                        ins = DVE.tensor_tensor(out=actT[:, hb, :T], in0=cbuf[ci][:, :T], in1=ps[bv][:, :T], op=ALU.mult)
                        C.bank_free(ins, bv)
                        C.free[("ffg", ci)] = C.evt(f"bf{bv}")
                        act_done = C.evt(f"bf{bv}")
                C.wait_evt(PE, act_done)
                for fc in range(32):
                    b = 4 + (fc % 3)
                    C.bank_begin(b)
                    for half in range(2):
                        i = half
                        C.wait_evt(POOL, C.free.get(("ffwu", i)))
                        C.inc(POOL.dma_start(out=w2v[i], in_=w2[:, half * 43:(half + 1) * 43, fc * 128:(fc + 1) * 128]), f"wl{i}", 16)
                        C.wait(PE, f"wl{i}")
                        for kk in range(43):
                            kc = half * 43 + kk
                            ins = PE.matmul(ps[b][:, :T], w2v[i][:, kk, :], actT[:, kc, :T], start=(kc == 0), stop=(kc == HB - 1))
                        if half == 0:
                            C.inc(ins, "ffw2h")
                            C.free[("ffwu", i)] = C.evt("ffw2h")
                    C.bank_ready(ins, b)
                    C.free[("ffwu", 1)] = C.bank_evt(b)
                    C.free["ffact"] = C.bank_evt(b)
                    epi2(fc, ti, (t0, T, r), b)
            dq.flush()
            C.barrier()

    for l in range(DEPTH):
        phase_mods(l)
    if stop_after == "mods":
        return nc, C
    phase_t0()
    if stop_after == "t0":
        return nc, C
    for l in range(DEPTH):
        if l == 0:
            ln_pass(l, xT, "mod1")
            if stop_after == "ln0":
                return nc, C
        phase_win(l)
        if stop_after == "win":
            return nc, C
        phase_fourier_a(l)
        if stop_after == "fa":
            return nc, C
        phase_fourier_b(l)
        if stop_after == "fb":
            return nc, C
        phase_ssm(l)
        if stop_after == "ssm":
            return nc, C
        phase_glu(l)
        if stop_after == "glu":
            return nc, C
        phase_wout(l)
        if stop_after == "wout":
            return nc, C
        ln_pass(l, pre, "post1")
        if stop_after == "post1":
            return nc, C
        phase_ffn(l)
        if stop_after == "ffn":
            return nc, C
        ln_pass(l, pre, "post2" if l + 1 < DEPTH else "final")
        if stop_after == "post2":
            return nc, C
    return nc, C


def host_consts():
    l = np.arange(2048, dtype=np.int64)
    prod = (l[:, None] * l[None, :]) % 2048
    ang = prod.astype(np.float64) * (2.0 * np.pi / 2048.0)
    cosT = np.cos(ang).astype(np.float32)
    sinT = np.sin(ang).astype(np.float32)
    q = np.arange(NT)
    iota_f = np.where(q < LX, q + LC, q - LX).astype(np.float32)
    iota_b = (NT - 1 - q).astype(np.float32)
    iota2 = np.stack([iota_f, iota_b]).astype(np.float32)
    return cosT, sinT, iota2, np.eye(128, dtype=np.float32)


def make_in_maps(inputs, cores):
    cosT, sinT, iota2, ident = host_consts()
    maps = []
    for c in cores:
        b = c % 4
        m = {
            "xb": np.ascontiguousarray(inputs["x"][b]),
            "ctxb": np.ascontiguousarray(inputs["ctx"][b]),
            "cvec": np.ascontiguousarray(np.stack([inputs["c"][b], inputs["c_ctx"]])),
            "ident_in": ident, "cosT": cosT, "sinT": sinT, "iota2": iota2,
        }
        for k in WSHAPES:
            m[k] = np.ascontiguousarray(inputs[k])
        maps.append(m)
    return maps


def kernel(**inputs):
    inputs = {k: np.asarray(v, dtype=np.float32) for k, v in inputs.items()}
    nc, C = build()
    cores = list(range(8))
    res = run_bass_kernel_spmd(nc, make_in_maps(inputs, cores), core_ids=cores)
    out = np.stack([res.results[b]["out"] for b in range(4)], axis=0)
    return out.astype(np.float32)
```

```python
import os
import math
import numpy as np
from contextlib import ExitStack
import concourse.bass as bass
import concourse.mybir as mybir
from concourse.bass_utils import run_bass_kernel_spmd

F32 = mybir.dt.float32
BF16 = mybir.dt.bfloat16
AF = mybir.ActivationFunctionType
ALU = mybir.AluOpType

D = 4096
KC = 32
LX = 2048
LC = 256
NT = LX + LC
DEPTH = 2
FFN = 11008
HB = FFN // 128
IN_COLS = 12288
ALPHA = (2.0 * DEPTH) ** 0.25
EPS = 1e-6
MAGIC = 12582912.0
TWO_PI_S = 6.28318
TILES = [(0, 512, 0), (512, 512, 0), (1024, 512, 0), (1536, 512, 0), (2048, 256, 1)]

WSHAPES = {
    "ada_w": [DEPTH, D, 6 * D], "ada_b": [DEPTH, 6 * D], "w_in": [DEPTH, D, IN_COLS],
    "fourier_w": [DEPTH, 2048, D], "ssm_a_re": [DEPTH, 2, 128, 64], "ssm_a_im": [DEPTH, 2, 128, 64],
    "ssm_log_dt": [DEPTH, 2, 128], "ssm_b_re": [DEPTH, 2, 128, 64, 16], "ssm_b_im": [DEPTH, 2, 128, 64, 16],
    "ssm_c_re": [DEPTH, 2, 128, 16, 64], "ssm_c_im": [DEPTH, 2, 128, 16, 64], "ssm_d": [DEPTH, 2048],
    "glu_w": [DEPTH, 2048, 2 * D], "w_out": [DEPTH, D, D], "ln1_g": [DEPTH, D], "ln1_b": [DEPTH, D],
    "ffn_w12": [DEPTH, D, 2 * FFN], "ffn_conv_w": [DEPTH, 3, FFN], "ffn_conv_b": [DEPTH, FFN],
    "ffn_w2": [DEPTH, FFN, D], "ln2_g": [DEPTH, D], "ln2_b": [DEPTH, D],
}


class Ctx:
    def __init__(self, nc):
        self.nc = nc
        self.es = ExitStack()
        self.S = {}
        self.bank_uses = [0] * 8
        self.uses = {}
        self.free = {}
        self._inced = set()
        self.engines = [nc.sync, nc.scalar, nc.vector, nc.gpsimd, nc.tensor]

    def sem(self, name):
        if name not in self.S:
            self.S[name] = [self.es.enter_context(self.nc.semaphore(name)), 0]
        return self.S[name]

    def inc(self, ins, name, n=1):
        assert id(ins) not in self._inced, "instruction already has a then_inc"
        self._inced.add(id(ins))
        self._keep = getattr(self, "_keep", [])
        self._keep.append(ins)
        e = self.sem(name)
        ins.then_inc(e[0], n)
        e[1] += n
        return e[1]

    def cnt(self, name):
        return self.sem(name)[1]

    def evt(self, name):
        return (name, self.sem(name)[1])

    def wait_evt(self, eng, ev):
        if ev is None:
            return
        if isinstance(ev, list):
            for e in ev:
                self.wait_evt(eng, e)
            return
        self.wait(eng, ev[0], ev[1])

    def bank_evt(self, b):
        return (f"br{b}", self.bank_uses[b])

    def wait(self, eng, name, val=None):
        e = self.sem(name)
        v = e[1] if val is None else val
        if v > 0:
            eng.wait_ge(e[0], v)

    def barrier(self):
        for eng in self.engines:
            for name, (h, c) in self.S.items():
                if c > 0:
                    eng.wait_ge(h, c)

    def use(self, key):
        k = self.uses.get(key, 0)
        self.uses[key] = k + 1
        return k

    def bank_begin(self, b):
        k = self.bank_uses[b]
        self.wait(self.nc.tensor, f"bf{b}", k)

    def bank_ready(self, ins, b):
        self.inc(ins, f"br{b}")
        self.bank_uses[b] += 1

    def bank_wait(self, eng, b):
        self.wait(eng, f"br{b}", self.bank_uses[b])

    def bank_free(self, ins, b):
        self.inc(ins, f"bf{b}")


def build(debug_outs=(), stop_after=None):
    nc = bass.Bass("TRN2", target_bir_lowering=False)
    C = Ctx(nc)
    PE, ACT, DVE, POOL, SP = nc.tensor, nc.scalar, nc.vector, nc.gpsimd, nc.sync

    def din(name, shape):
        return nc.dram_tensor(name, shape, F32, kind="ExternalInput").ap()

    xb = din("xb", [LX, D])
    ctxb = din("ctxb", [LC, D])
    cvec = din("cvec", [2, D])
    W = {k: din(k, s) for k, s in WSHAPES.items()}
    ident_d = din("ident_in", [128, 128])
    cos_d = din("cosT", [2048, 2048])
    sin_d = din("sinT", [2048, 2048])
    iota_d = din("iota2", [2, NT])
    out_d = nc.dram_tensor("out", [LX, D], F32, kind="ExternalOutput").ap()

    def scr(name, shape, dt):
        kind = "ExternalOutput" if name in debug_outs else "Internal"
        return nc.dram_tensor(name, shape, dt, kind=kind).ap()

    xT = scr("xT", [D, NT], F32)
    pre = scr("pre", [D, NT], F32)
    hT = scr("hT", [D, NT], BF16)
    ufS = scr("ufS", [2048, NT], BF16)
    usS = scr("usS", [2048, NT], F32)
    gtS = scr("gtS", [8192, NT], BF16)
    fTS = scr("fTS", [2048, NT], BF16)
    m1S = scr("m1S", [D, NT], F32)
    ygS = scr("ygS", [2048, NT], BF16)
    mgS = scr("mgS", [D, NT], BF16)
    BsS = scr("BsS", [2, 128, 2 * 16 * 128], F32)
    modsS = scr("modsS", [DEPTH, 128, 192 * 2], F32)

    def fm(ap):
        return ap.rearrange("(k p) t -> p k t", p=128)

    es = C.es
    ps = [es.enter_context(nc.psum_tensor(f"ps{i}", [128, 512], F32)) for i in range(8)]
    _uid = [0]

    def sb(stack, name, shape, dt):
        _uid[0] += 1
        return stack.enter_context(nc.sbuf_tensor(f"{name}_u{_uid[0]}", shape, dt))

    ident = sb(es, "ident", [128, 128], F32)
    ones_b = sb(es, "ones_b", [128, 128], BF16)
    mods = [sb(es, f"mods{l}", [128, 192, 2], F32) for l in range(DEPTH)]
    lnp = [sb(es, f"lnp{l}", [128, 4, 32], F32) for l in range(DEPTH)]
    halfpi = sb(es, "halfpi", [128, 1], F32)
    epsT = sb(es, "epsT", [128, 1], F32)
    sgn = sb(es, "sgn", [128, 2], F32)
    magP = sb(es, "magP", [128, 1], F32)
    magN = sb(es, "magN", [128, 1], F32)
    vstage = sb(es, "vstage", [128, 128], F32)

    C.inc(SP.dma_start(out=ident[:], in_=ident_d[:, :]), "init", 16)
    C.inc(DVE.memset(ones_b[:], 1.0), "initv")
    C.inc(DVE.memset(halfpi[:], math.pi / 2), "initv")
    C.inc(DVE.memset(epsT[:], EPS), "initv")
    C.inc(DVE.memset(magP[:], MAGIC), "initv")
    C.inc(DVE.memset(magN[:], -MAGIC), "initv")
    C.inc(DVE.memset(sgn[:], -1.0), "initv")
    C.wait(DVE, "initv")
    C.inc(DVE.memset(sgn[0:64, 0:1], 1.0), "initv")
    C.barrier()

    def pe_mark(name):
        C.inc(PE.matmul(ps[7][0:2, 510:512], ones_b[:, 0:2], ones_b[:, 0:2], start=True, stop=True), name)
        return C.evt(name)

    def load_vec_fm(dst, src_ap, nch):
        done = 0
        while done < nch:
            n = min(128, nch - done)
            C.wait_evt(SP, C.free.get("vstage"))
            C.inc(SP.dma_start(out=vstage[0:n, :], in_=src_ap[done * 128:(done + n) * 128].rearrange("(c p) -> c p", p=128)), "vld", 16)
            C.wait(PE, "vld")
            C.bank_begin(7)
            C.bank_ready(PE.transpose(ps[7][:, 0:n], vstage[0:n, :], ident[0:n, 0:n]), 7)
            C.free["vstage"] = C.bank_evt(7)
            C.bank_wait(DVE, 7)
            C.bank_free(DVE.tensor_copy(out=dst[:, done:done + n], in_=ps[7][:, 0:n]), 7)
            done += n

    class Deferred:
        def __init__(self):
            self.q = []

        def push(self, fn):
            self.q.append(fn)

        def flush(self, keep=0):
            while len(self.q) > keep:
                self.q.pop(0)()

    class Stage:
        def __init__(self, stack, name, n, shape, dt):
            self.role = "SA" if dt == F32 else "SB"
            self.bufs = [sb(stack, f"{name}{i}", shape, dt) for i in range(n)]
            self.n = n
            self.c = 0
            self.base = [C.cnt(f"{self.role}{i}") for i in range(n)]

        def get(self, eng):
            i = self.c % self.n
            k = self.c // self.n
            self.c += 1
            C.wait(eng, f"{self.role}{i}", self.base[i] + 16 * k)
            return i, self.bufs[i]

        def stsem(self, i):
            return f"{self.role}{i}"

    class Loader:
        def __init__(self, stack, name, n, shape, dt):
            self.name = "LA" if dt == F32 else "LB"
            self.bufs = [sb(stack, f"{name}{i}", shape, dt) for i in range(n)]
            self.n = n
            self.c = 0
            self.pending = []
            self.freeev = [None] * n
            self.base = [C.cnt(f"{self.name}l{i}") for i in range(n)]

        def issue(self, dst_fn, src):
            i = self.c % self.n
            k = self.c // self.n
            self.c += 1
            C.wait_evt(SP, self.freeev[i])
            C.inc(SP.dma_start(out=dst_fn(self.bufs[i]), in_=src), f"{self.name}l{i}", 16)
            self.pending.append((i, k))

        def take(self, eng):
            i, k = self.pending.pop(0)
            C.wait(eng, f"{self.name}l{i}", self.base[i] + 16 * (k + 1))
            return i, self.bufs[i]

        def release(self, i, ev):
            self.freeev[i] = ev

    def gemm(name, act, KCn, wview, nblk, blkw, epi, wbufs, ntiles=TILES, banks=7):
        nsub = blkw // 128
        bsel = 0
        for j in range(nblk):
            i = j % len(wbufs)
            C.wait_evt(POOL, C.free.get((name, i)))
            for (dst, srcap) in wview(j, wbufs[i]):
                C.inc(POOL.dma_start(out=dst, in_=srcap), f"wl{i}", 16)
            C.wait(PE, f"wl{i}")
            for sub in range(nsub):
                fc = j * nsub + sub
                for ti, (t0, T, r) in enumerate(ntiles):
                    b = bsel % banks
                    bsel += 1
                    C.bank_begin(b)
                    for kc in range(KCn):
                        ins = PE.matmul(ps[b][:, :T], wbufs[i][:, kc, sub * 128:(sub + 1) * 128], act[:, kc, t0:t0 + T],
                                        start=(kc == 0), stop=(kc == KCn - 1))
                    C.bank_ready(ins, b)
                    C.free[(name, i)] = C.bank_evt(b)
                    epi(fc, ti, (t0, T, r), b)

    def phase_mods(l):
        with ExitStack() as ph:
            wb = [sb(ph, f"mwb{i}", [128, 32, 256], BF16) for i in range(2)]
            cs = sb(ph, "cs", [128, 2, 32], F32)
            csb = sb(ph, "csb", [128, 2, 32], BF16)
            brow = sb(ph, "brow", [1, 6 * D], F32)
            ones2 = sb(ph, "ones2", [1, 2], F32)
            for r in range(2):
                C.inc(SP.dma_start(out=cs[:, r, :], in_=cvec[r].rearrange("(p k) -> p k", k=32)), "mld", 16)
            C.inc(SP.dma_start(out=brow[:], in_=W["ada_b"][l:l + 1, :]), "mld", 16)
            C.inc(DVE.memset(ones2[:], 1.0), "mset")
            C.wait(ACT, "mld")
            C.inc(ACT.activation(out=csb[:], in_=cs[:], func=AF.Silu), "mset")
            C.wait(PE, "mset")
            C.wait(PE, "mld")
            wv = W["ada_w"][l].rearrange("(p k) n -> p k n", k=32)
            NB = 6 * D // 256
            for j in range(NB):
                i = j % 2
                C.wait_evt(POOL, C.free.get(("mw", i)))
                C.inc(POOL.dma_start(out=wb[i][:], in_=wv[:, :, j * 256:(j + 1) * 256]), f"wl{i}", 16)
                C.wait(PE, f"wl{i}")
                for sub in range(2):
                    nch = j * 2 + sub
                    b = nch % 4
                    C.bank_begin(b)
                    for kc in range(32):
                        PE.matmul(ps[b][:, 0:2], wb[i][:, kc, sub * 128:(sub + 1) * 128], csb[:, :, kc], start=(kc == 0), stop=False)
                    ins = PE.matmul(ps[b][:, 0:2], brow[0:1, nch * 128:(nch + 1) * 128], ones2[0:1, :], start=False, stop=True)
                    C.bank_ready(ins, b)
                    C.free[("mw", i)] = C.bank_evt(b)
                    C.bank_wait(DVE, b)
                    C.bank_free(DVE.tensor_copy(out=mods[l][:, nch, :], in_=ps[b][:, 0:2]), b)
            DVE.tensor_scalar(out=mods[l][:, 32:64, :], in0=mods[l][:, 32:64, :], scalar1=1.0, scalar2=None, op0=ALU.add)
            C.inc(DVE.tensor_scalar(out=mods[l][:, 128:160, :], in0=mods[l][:, 128:160, :], scalar1=1.0, scalar2=None, op0=ALU.add), "mset")
            for i, nm in enumerate(["ln1_g", "ln1_b", "ln2_g", "ln2_b"]):
                load_vec_fm(lnp[l][:, i, :], W[nm][l], 32)
            if "modsS" in debug_outs:
                C.wait(SP, "mset")
                C.inc(SP.dma_start(out=modsS[l], in_=mods[l][:].rearrange("p a b -> p (a b)")), "mld", 16)
            C.barrier()

    def phase_t0():
        with ExitStack() as ph:
            xin = [sb(ph, f"xin{i}", [128, D], F32) for i in range(2)]
            xst = [sb(ph, f"xst{i}", [128, 32, 128], F32) for i in range(2)]
            xTv = fm(xT)
            for it in range(18):
                i = it % 2
                k = it // 2
                src = xb[it * 128:(it + 1) * 128, :] if it < 16 else ctxb[(it - 16) * 128:(it - 15) * 128, :]
                C.wait_evt(SP, C.free.get(("xin", i)))
                C.inc(SP.dma_start(out=xin[i][:], in_=src), f"xinl{i}", 16)
                C.wait(PE, f"xinl{i}")
                C.wait(DVE, f"xsts{i}", 16 * k)
                C.wait(ACT, f"xsts{i}", 16 * k)
                evs = []
                for q in range(8):
                    b = q % 7
                    C.bank_begin(b)
                    for rr in range(4):
                        kc = q * 4 + rr
                        ins = PE.transpose(ps[b][:, rr * 128:(rr + 1) * 128], xin[i][:, kc * 128:(kc + 1) * 128], ident[:])
                    C.bank_ready(ins, b)
                    C.free[("xin", i)] = C.bank_evt(b)
                    eng = DVE if q % 2 == 0 else ACT
                    C.bank_wait(eng, b)
                    o = xst[i][:, q * 4:(q + 1) * 4, :]
                    iv = ps[b][:].rearrange("p (r t) -> p r t", t=128)
                    if eng is DVE:
                        ins = DVE.tensor_copy(out=o, in_=iv)
                    else:
                        ins = ACT.activation(out=o, in_=iv, func=AF.Copy)
                    C.bank_free(ins, b)
                    evs.append(C.evt(f"bf{b}"))
                C.wait_evt(SP, evs)
                C.inc(SP.dma_start(out=xTv[:, :, it * 128:(it + 1) * 128], in_=xst[i][:]), f"xsts{i}", 16)
            C.barrier()

    def ln_pass(l, src, kind, tiles=TILES):
        with ExitStack() as ph:
            yb = sb(ph, "ln_y", [128, 32, 512], F32)
            sqb = sb(ph, "ln_sq", [128, 32, 512], BF16)
            ybb = sb(ph, "ln_yb", [128, 32, 512], BF16)
            hob = sb(ph, "ln_ho", [128, 32, 512], BF16) if kind != "final" else None
            mean_t = sb(ph, "ln_mean", [128, 512], F32)
            rstd_t = sb(ph, "ln_rstd", [128, 512], F32)
            tmp_t = sb(ph, "ln_tmp", [128, 512], F32)
            ost = [sb(ph, f"ln_ost{i}", [128, 1024], F32) for i in range(2)] if kind == "final" else None
            srcv = fm(src)
            xTv = fm(xT)
            hTv = fm(hT)
            reads = []

            def stats_and_norm(T):
                b1, b2 = 0, 1
                C.bank_begin(b1)
                C.bank_begin(b2)
                for kc in range(32):
                    C.inc(ACT.activation(out=sqb[:, kc, :T], in_=yb[:, kc, :T], func=AF.Square), "lnsq")
                    C.inc(POOL.tensor_copy(out=ybb[:, kc, :T], in_=yb[:, kc, :T]), "lncp")
                    C.wait(PE, "lnsq")
                    C.wait(PE, "lncp")
                    i1 = PE.matmul(ps[b1][:, :T], ones_b[:], ybb[:, kc, :T], start=(kc == 0), stop=(kc == 31))
                    i2 = PE.matmul(ps[b2][:, :T], ones_b[:], sqb[:, kc, :T], start=(kc == 0), stop=(kc == 31))
                C.bank_ready(i1, b1)
                C.bank_ready(i2, b2)
                C.bank_wait(DVE, b1)
                C.bank_wait(DVE, b2)
                C.bank_free(DVE.tensor_scalar(out=mean_t[:, :T], in0=ps[b1][:, :T], scalar1=1.0 / D, scalar2=None, op0=ALU.mult), b1)
                DVE.tensor_tensor(out=tmp_t[:, :T], in0=mean_t[:, :T], in1=mean_t[:, :T], op=ALU.mult)
                ins = DVE.scalar_tensor_tensor(out=tmp_t[:, :T], in0=ps[b2][:, :T], scalar=1.0 / D, in1=tmp_t[:, :T], op0=ALU.mult, op1=ALU.subtract)
                C.bank_free(ins, b2)
                C.wait(ACT, f"bf{b2}")
                C.inc(ACT.activation(out=tmp_t[:, :T], in_=tmp_t[:, :T], func=AF.Sqrt, bias=epsT[:, 0:1], scale=1.0), "lnsd")
                C.wait(DVE, "lnsd")
                DVE.reciprocal(out=rstd_t[:, :T], in_=tmp_t[:, :T])
                for kc in range(32):
                    DVE.tensor_tensor(out=yb[:, kc, :T], in0=yb[:, kc, :T], in1=mean_t[:, :T], op=ALU.subtract)
                    C.inc(DVE.tensor_tensor(out=yb[:, kc, :T], in0=yb[:, kc, :T], in1=rstd_t[:, :T], op=ALU.mult), "lnn")

            for (t0, T, r) in tiles:
                if kind == "final" and r == 1:
                    continue
                C.wait_evt(SP, reads)
                reads = []
                C.inc(SP.dma_start(out=yb[:, :, :T], in_=srcv[:, :, t0:t0 + T]), "lnyl", 16)
                for e in (ACT, POOL, DVE):
                    C.wait(e, "lnyl")
                nn0 = C.cnt("lnn")
                stats_and_norm(T)
                if kind == "mod1":
                    steps = [("mod", l, 0)]
                elif kind == "post1":
                    steps = [("aff", l, 0), ("mod", l, 3)]
                elif kind == "post2":
                    steps = [("aff", l, 2)] + ([("mod", l + 1, 0)] if l + 1 < DEPTH else [])
                else:
                    steps = [("aff", l, 2)]
                for si, (st, ll, mi) in enumerate(steps):
                    if st == "aff":
                        for kc in range(32):
                            C.wait(ACT, "lnn", nn0 + kc + 1)
                            ins = ACT.activation(out=yb[:, kc, :T], in_=yb[:, kc, :T], func=AF.Identity,
                                                 bias=lnp[ll][:, mi + 1, kc:kc + 1], scale=lnp[ll][:, mi, kc:kc + 1])
                        C.inc(ins, "lnaf")
                        if kind == "final":
                            C.wait(PE, "lnaf")
                            oc = 0
                            for sub in range(T // 128):
                                for half in range(4):
                                    kq = C.use("ost")
                                    oi = kq % 2
                                    C.wait(DVE, f"osts{oi}", 16 * (kq // 2))
                                    for q in range(2):
                                        b = 2 + (half * 2 + q) % 4
                                        C.bank_begin(b)
                                        for rr in range(4):
                                            kc = half * 8 + q * 4 + rr
                                            ins = PE.transpose(ps[b][:, rr * 128:(rr + 1) * 128], yb[:, kc, sub * 128:(sub + 1) * 128], ident[:])
                                        C.bank_ready(ins, b)
                                        pe_last = C.bank_evt(b)
                                        C.bank_wait(DVE, b)
                                        ins = DVE.tensor_copy(out=ost[oi][:, q * 512:(q + 1) * 512], in_=ps[b][:])
                                        C.bank_free(ins, b)
                                    C.wait(SP, f"bf{b}")
                                    C.inc(SP.dma_start(out=out_d[t0 + sub * 128:t0 + (sub + 1) * 128, half * 1024:(half + 1) * 1024], in_=ost[oi][:]), f"osts{oi}", 16)
                            reads.append(pe_last)
                        else:
                            C.wait(SP, "lnaf")
                            C.inc(SP.dma_start(out=xTv[:, :, t0:t0 + T], in_=yb[:, :, :T]), "lnxs", 16)
                            reads.append(C.evt("lnxs"))
                        if si + 1 < len(steps):
                            C.wait(DVE, "lnxs")
                            C.wait(POOL, "lnaf")
                            nn0 = C.cnt("lnn")
                            stats_and_norm(T)
                    else:
                        kk = C.use("lnho")
                        C.wait(ACT, "lnhs", 16 * kk)
                        for kc in range(32):
                            C.wait(ACT, "lnn", nn0 + kc + 1)
                            ins = ACT.activation(out=hob[:, kc, :T], in_=yb[:, kc, :T], func=AF.Identity,
                                                 bias=mods[ll][:, mi * 32 + kc, r:r + 1], scale=mods[ll][:, (mi + 1) * 32 + kc, r:r + 1])
                        C.inc(ins, "lnmo")
                        C.wait(SP, "lnmo")
                        C.inc(SP.dma_start(out=hTv[:, :, t0:t0 + T], in_=hob[:, :, :T]), "lnhs", 16)
                        reads.append(C.evt("lnmo"))
                reads += [C.evt("lnsq"), C.evt("lncp"), C.evt("lnn"), C.evt("lnaf")]
            C.barrier()

    def phase_win(l):
        with ExitStack() as ph:
            act = sb(ph, "win_act", [128, 32, NT], BF16)
            wb = [sb(ph, f"win_wb{i}", [128, 32, 256], BF16) for i in range(2)]
            sf = Stage(ph, "winsf", 3, [128, 512], F32)
            sbf = Stage(ph, "winsb", 3, [128, 512], BF16)
            hTv = fm(hT)
            for q in range(4):
                C.inc(SP.dma_start(out=act[:, q * 8:(q + 1) * 8, :], in_=hTv[:, q * 8:(q + 1) * 8, :]), "actl", 16)
            C.wait(PE, "actl")
            wv = W["w_in"][l].rearrange("(k p) n -> p k n", p=128)
            ufv, usv, gtv = fm(ufS), fm(usS), fm(gtS)

            def wview(j, buf):
                return [(buf[:], wv[:, :, j * 256:(j + 1) * 256])]

            def epi(fc, ti, tl, b):
                t0, T, r = tl
                C.bank_wait(ACT, b)
                if fc < 16:
                    stg, fn, dst = sbf, AF.Copy, ufv[:, fc, t0:t0 + T]
                elif fc < 32:
                    stg, fn, dst = sf, AF.Copy, usv[:, fc - 16, t0:t0 + T]
                else:
                    stg, fn, dst = sbf, AF.Sigmoid, gtv[:, fc - 32, t0:t0 + T]
                i, st = stg.get(ACT)
                ins = ACT.activation(out=st[:, :T], in_=ps[b][:, :T], func=fn)
                C.bank_free(ins, b)
                C.wait(ACT, f"bf{b}")
                C.inc(ACT.dma_start(out=dst, in_=st[:, :T]), stg.stsem(i), 16)

            gemm("win", act, 32, wview, IN_COLS // 256, 256, epi, wbufs=wb)
            C.barrier()

    def phase_fourier_a(l):
        with ExitStack() as ph:
            CT = sb(ph, "fa_CT", [128, 16, 2048], BF16)
            ST = sb(ph, "fa_ST", [128, 16, 2048], BF16)
            UC = sb(ph, "fa_UC", [128, 16, 512], BF16)
            US = sb(ph, "fa_US", [128, 16, 512], BF16)
            uft = [sb(ph, f"fa_uf{i}", [128, 4, 2048], BF16) for i in range(2)]
            sbf = Stage(ph, "fasb", 3, [128, 512], BF16)
            cv = cos_d.rearrange("(c p) l -> p c l", p=128)
            sv = sin_d.rearrange("(c p) l -> p c l", p=128)
            for q in range(4):
                C.inc(POOL.dma_start(out=CT[:, q * 4:(q + 1) * 4, :], in_=cv[:, q * 4:(q + 1) * 4, :]), "actl", 16)
                C.inc(POOL.dma_start(out=ST[:, q * 4:(q + 1) * 4, :], in_=sv[:, q * 4:(q + 1) * 4, :]), "actl", 16)
            C.wait(PE, "actl")
            ufv = fm(ufS)
            fTv = fm(fTS)
            it = 0
            s2_done = None
            for (s0, L) in ((0, LX), (LX, LC)):
                ntc = L // 128
                step = 2048 // L
                for g in range(4):
                    i = it % 2
                    it += 1
                    C.wait_evt(SP, C.free.get(("fauf", i)))
                    C.inc(SP.dma_start(out=uft[i][:, :, :L], in_=ufv[:, g * 4:(g + 1) * 4, s0:s0 + L]), f"faul{i}", 16)
                    C.wait(PE, f"faul{i}")
                    C.wait_evt(DVE, s2_done)
                    C.wait_evt(ACT, s2_done)
                    evs = []
                    for tc in range(ntc):
                        for which, (TAB, DST) in enumerate(((CT, UC), (ST, US))):
                            b = (tc * 2 + which) % 6
                            C.bank_begin(b)
                            for chc in range(4):
                                ins = PE.matmul(ps[b][:, :], uft[i][:, chc, tc * 128:(tc + 1) * 128], TAB[:, chc, 0:2048:4],
                                                start=(chc == 0), stop=(chc == 3))
                            C.bank_ready(ins, b)
                            C.free[("fauf", i)] = C.bank_evt(b)
                            if which == 0:
                                C.bank_wait(DVE, b)
                                ins = DVE.tensor_copy(out=UC[:, tc, :], in_=ps[b][:, :])
                            else:
                                C.bank_wait(ACT, b)
                                ins = ACT.activation(out=US[:, tc, :], in_=ps[b][:, :], func=AF.Copy, scale=-1.0)
                            C.bank_free(ins, b)
                    for b in range(6):
                        C.wait(PE, f"bf{b}")
                    sc = 1.0 / math.sqrt(L * 512.0)
                    nlt = max(1, L // 512)
                    TL = min(L, 512)
                    for kq in range(4):
                        for lt in range(nlt):
                            b = (kq * nlt + lt) % 6
                            C.bank_begin(b)
                            for tc in range(ntc):
                                if step == 1:
                                    rc = CT[:, tc, lt * 512:(lt + 1) * 512]
                                    rs = ST[:, tc, lt * 512:(lt + 1) * 512]
                                else:
                                    rc = CT[:, tc, 0:2048:step]
                                    rs = ST[:, tc, 0:2048:step]
                                PE.matmul(ps[b][:, :TL], UC[:, tc, kq * 128:(kq + 1) * 128], rc, start=(tc == 0), stop=False)
                                ins = PE.matmul(ps[b][:, :TL], US[:, tc, kq * 128:(kq + 1) * 128], rs, start=False, stop=(tc == ntc - 1))
                            C.bank_ready(ins, b)
                            s2_done = C.bank_evt(b)
                            C.bank_wait(ACT, b)
                            si, st = sbf.get(ACT)
                            ins = ACT.activation(out=st[:, :TL], in_=ps[b][:, :TL], func=AF.Copy, scale=sc)
                            C.bank_free(ins, b)
                            C.wait(ACT, f"bf{b}")
                            C.inc(ACT.dma_start(out=fTv[:, g * 4 + kq, s0 + lt * 512:s0 + lt * 512 + TL], in_=st[:, :TL]), sbf.stsem(si), 16)
            C.barrier()

    def phase_fourier_b(l, tiles=TILES):
        with ExitStack() as ph:
            act = sb(ph, "fb_act", [128, 16, NT], BF16)
            wb = [sb(ph, f"fb_wb{i}", [128, 16, 512], BF16) for i in range(2)]
            sf = Stage(ph, "fbsf", 3, [128, 512], F32)
            gl = Loader(ph, "fbg", 3, [128, 512], BF16)
            fTv = fm(fTS)
            for q in range(2):
                C.inc(SP.dma_start(out=act[:, q * 8:(q + 1) * 8, :], in_=fTv[:, q * 8:(q + 1) * 8, :]), "actl", 16)
            C.wait(PE, "actl")
            wv = W["fourier_w"][l].rearrange("(k p) n -> p k n", p=128)
            gtv, m1v = fm(gtS), fm(m1S)

            def wview(j, buf):
                return [(buf[:], wv[:, :, j * 512:(j + 1) * 512])]

            def epi(fc, ti, tl, b):
                t0, T, r = tl
                gl.issue(lambda bf: bf[:, :T], gtv[:, fc, t0:t0 + T])
                gi, gb = gl.take(DVE)
                C.bank_wait(DVE, b)
                si, st = sf.get(DVE)
                ins = DVE.tensor_tensor(out=st[:, :T], in0=ps[b][:, :T], in1=gb[:, :T], op=ALU.mult)
                C.bank_free(ins, b)
                gl.release(gi, C.evt(f"bf{b}"))
                C.wait(ACT, f"bf{b}")
                C.inc(ACT.dma_start(out=m1v[:, fc, t0:t0 + T], in_=st[:, :T]), sf.stsem(si), 16)

            gemm("fb", act, 16, wview, D // 512, 512, epi, wbufs=wb, ntiles=tiles)
            C.barrier()

    def phase_ssm_prep(l, RT, TH, CM, Dv):
        with ExitStack() as ph:
            are = sb(ph, "sp_are", [128, 64], F32)
            aim = sb(ph, "sp_aim", [128, 64], F32)
            ldt = sb(ph, "sp_ldt", [128, 1], F32)
            bre = sb(ph, "sp_bre", [128, 1024], F32)
            bim = sb(ph, "sp_bim", [128, 1024], F32)
            t = [sb(ph, f"sp_t{i}", [128, 64], F32) for i in range(14)]
            RR = sb(ph, "sp_RR", [128, 128], F32)
            TT = sb(ph, "sp_TT", [128, 128], F32)
            BB = sb(ph, "sp_BB", [128, 2, 16, 128], F32)
            big = [sb(ph, f"sp_big{i}", [128, 16, 64], F32) for i in range(4)]
            CC = [sb(ph, f"sp_CC{i}", [128, 128], F32) for i in range(2)]
            load_vec_fm(Dv[:, :], W["ssm_d"][l], 16)
            for d in range(2):
                C.wait(SP, "spv")
                C.wait(SP, "spa")
                C.inc(SP.dma_start(out=are[:], in_=W["ssm_a_re"][l, d]), "spl", 16)
                C.inc(SP.dma_start(out=aim[:], in_=W["ssm_a_im"][l, d]), "spl", 16)
                C.inc(SP.dma_start(out=ldt[:], in_=W["ssm_log_dt"][l, d].rearrange("(g o) -> g o", o=1)), "spl", 16)
                C.inc(SP.dma_start(out=bre[:], in_=W["ssm_b_re"][l, d].rearrange("g p c -> g (p c)")), "spl", 16)
                C.inc(SP.dma_start(out=bim[:], in_=W["ssm_b_im"][l, d].rearrange("g p c -> g (p c)")), "spl", 16)
                C.wait(ACT, "spl")
                C.wait(DVE, "spl")
                dt_, lr, thr, rho, fr, sn, cs_, abr, abi, x0, nr, den, cfr, cfi = t
                C.inc(ACT.activation(out=dt_[:, 0:1], in_=ldt[:], func=AF.Exp), "spa")
                C.wait(DVE, "spa")
                DVE.tensor_scalar(out=lr[:], in0=are[:], scalar1=dt_[:, 0:1], scalar2=None, op0=ALU.mult)
                DVE.tensor_scalar(out=thr[:], in0=aim[:], scalar1=dt_[:, 0:1], scalar2=1.0 / (2 * math.pi), op0=ALU.mult, op1=ALU.mult)
                DVE.tensor_scalar(out=x0[:], in0=thr[:], scalar1=MAGIC, scalar2=None, op0=ALU.add)
                DVE.tensor_scalar(out=x0[:], in0=x0[:], scalar1=-MAGIC, scalar2=None, op0=ALU.add)
                C.inc(DVE.tensor_tensor(out=fr[:], in0=thr[:], in1=x0[:], op=ALU.subtract), "spv")
                C.wait(ACT, "spv")
                ACT.activation(out=rho[:], in_=lr[:], func=AF.Exp)
                ACT.activation(out=sn[:], in_=fr[:], func=AF.Sin, scale=TWO_PI_S)
                ACT.activation(out=x0[:], in_=fr[:], func=AF.Abs)
                C.inc(ACT.activation(out=cs_[:], in_=x0[:], func=AF.Sin, scale=-TWO_PI_S, bias=halfpi[:, 0:1]), "spa")
                C.wait(DVE, "spa")
                DVE.tensor_tensor(out=abr[:], in0=rho[:], in1=cs_[:], op=ALU.mult)
                DVE.tensor_tensor(out=abi[:], in0=rho[:], in1=sn[:], op=ALU.mult)
                for h in range(2):
                    DVE.tensor_copy(out=RR[:, h * 64:(h + 1) * 64], in_=rho[:])
                    DVE.tensor_copy(out=TT[:, h * 64:(h + 1) * 64], in_=thr[:])
                u1 = x0
                DVE.tensor_scalar(out=nr[:], in0=abr[:], scalar1=-1.0, scalar2=None, op0=ALU.add)
                DVE.tensor_tensor(out=den[:], in0=are[:], in1=are[:], op=ALU.mult)
                DVE.tensor_tensor(out=u1[:], in0=aim[:], in1=aim[:], op=ALU.mult)
                DVE.tensor_tensor(out=den[:], in0=den[:], in1=u1[:], op=ALU.add)
                DVE.reciprocal(out=den[:], in_=den[:])
                DVE.tensor_tensor(out=cfr[:], in0=nr[:], in1=are[:], op=ALU.mult)
                DVE.tensor_tensor(out=u1[:], in0=abi[:], in1=aim[:], op=ALU.mult)
                DVE.tensor_tensor(out=cfr[:], in0=cfr[:], in1=u1[:], op=ALU.add)
                DVE.tensor_tensor(out=cfr[:], in0=cfr[:], in1=den[:], op=ALU.mult)
                DVE.tensor_tensor(out=cfi[:], in0=abi[:], in1=are[:], op=ALU.mult)
                DVE.tensor_tensor(out=u1[:], in0=nr[:], in1=aim[:], op=ALU.mult)
                DVE.tensor_tensor(out=cfi[:], in0=cfi[:], in1=u1[:], op=ALU.subtract)
                DVE.tensor_tensor(out=cfi[:], in0=cfi[:], in1=den[:], op=ALU.mult)
                brv = bre[:].rearrange("g (p c) -> g c p", c=16)
                biv = bim[:].rearrange("g (p c) -> g c p", c=16)
                cfrb = cfr[:].rearrange("g (o p) -> g o p", o=1).to_broadcast([128, 16, 64])
                cfib = cfi[:].rearrange("g (o p) -> g o p", o=1).to_broadcast([128, 16, 64])
                C.wait(DVE, "spbs")
                DVE.tensor_tensor(out=big[0][:], in0=brv, in1=cfrb, op=ALU.mult)
                DVE.tensor_tensor(out=big[1][:], in0=biv, in1=cfib, op=ALU.mult)
                DVE.tensor_tensor(out=big[2][:], in0=biv, in1=cfrb, op=ALU.mult)
                DVE.tensor_tensor(out=big[3][:], in0=brv, in1=cfib, op=ALU.mult)
                DVE.tensor_tensor(out=BB[:, 0, :, 0:64], in0=big[0][:], in1=big[1][:], op=ALU.subtract)
                DVE.tensor_tensor(out=BB[:, 0, :, 64:128], in0=big[2][:], in1=big[3][:], op=ALU.add)
                DVE.tensor_copy(out=BB[:, 1, :, 0:64], in_=BB[:, 0, :, 64:128])
                C.inc(DVE.tensor_scalar(out=BB[:, 1, :, 64:128], in0=BB[:, 0, :, 0:64], scalar1=-1.0, scalar2=None, op0=ALU.mult), "spv")
                C.wait(SP, "spv")
                C.inc(SP.dma_start(out=BsS[d], in_=BB[:].rearrange("g v c q -> g (v c q)")), "spbs", 16)
                C.wait(PE, "spv")
                for src_t, dst_t in ((RR, RT), (TT, TH)):
                    C.bank_begin(6)
                    C.bank_ready(PE.transpose(ps[6][:, 0:128], src_t[:], ident[:]), 6)
                    C.bank_wait(DVE, 6)
                    C.bank_free(DVE.tensor_copy(out=dst_t[:, d, :], in_=ps[6][:, 0:128]), 6)
                crv = W["ssm_c_re"][l, d].rearrange("(cb g) co p -> cb (g co) p", g=8)
                civ = W["ssm_c_im"][l, d].rearrange("(cb g) co p -> cb (g co) p", g=8)
                for cb in range(16):
                    for v in range(2):
                        C.wait_evt(SP, C.free.get(("spcc", v)))
                        a, bsrc = (crv, civ) if v == 0 else (civ, crv)
                        C.inc(SP.dma_start(out=CC[v][:, 0:64], in_=a[cb]), f"spccl{v}", 16)
                        C.inc(SP.dma_start(out=CC[v][:, 64:128], in_=bsrc[cb]), f"spccl{v}", 16)
                        C.wait(PE, f"spccl{v}")
                        b = 4 + v
                        C.bank_begin(b)
                        C.bank_ready(PE.transpose(ps[b][:, 0:128], CC[v][:], ident[:]), b)
                        C.free[("spcc", v)] = C.bank_evt(b)
                        C.bank_wait(DVE, b)
                        ins = DVE.tensor_scalar(out=CM[:, d, cb * 8:(cb + 1) * 8, v, :],
                                                in0=ps[b][:, 0:128].rearrange("q (g co) -> q g co", co=16),
                                                scalar1=sgn[:, v:v + 1], scalar2=None, op0=ALU.mult)
                        C.bank_free(ins, b)
            C.barrier()

    def phase_ssm(l):
        with ExitStack() as ph:
            RT = sb(ph, "ss_RT", [128, 2, 128], F32)
            TH = sb(ph, "ss_TH", [128, 2, 128], F32)
            CM = sb(ph, "ss_CM", [128, 2, 128, 2, 16], BF16)
            Dv = sb(ph, "ss_Dv", [128, 16], F32)
            phase_ssm_prep(l, RT, TH, CM, Dv)
            iota = sb(ph, "ss_iota", [128, 2, NT], F32)
            ust = [sb(ph, f"ss_ust{i}", [128, NT], BF16) for i in range(2)]
            usf = [sb(ph, f"ss_usf{i}", [128, NT], F32) for i in range(2)]
            ZB = [sb(ph, f"ss_ZB{i}", [128, 8, 2, 128], BF16) for i in range(2)]
            ZC = [sb(ph, f"ss_ZC{i}", [128, 8, 2, 128], BF16) for i in range(2)]
            snT = [sb(ph, f"ss_sn{i}", [128, NT], F32) for i in range(2)]
            csT = [sb(ph, f"ss_cs{i}", [128, NT], F32) for i in range(2)]
            t1 = sb(ph, "ss_t1", [128, NT], F32)
            xxT = sb(ph, "ss_xx", [128, NT], F32)
            frT = sb(ph, "ss_fr", [128, NT], F32)
            abT = sb(ph, "ss_ab", [128, NT], F32)
            wT = sb(ph, "ss_w", [128, NT], F32)
            tmpT = sb(ph, "ss_tmp", [128, 512], F32)
            gT = sb(ph, "ss_g", [128, NT], F32)
            A1 = [sb(ph, f"ss_A1{i}", [128, NT], BF16) for i in range(2)]
            A2 = [sb(ph, f"ss_A2{i}", [128, NT], BF16) for i in range(2)]
            ytmp = sb(ph, "ss_yt", [128, 512], F32)
            ygs = [sb(ph, f"ss_yg{i}", [128, NT], BF16) for i in range(2)]
            for r in range(2):
                C.inc(SP.dma_start(out=iota[:, r, :], in_=iota_d[r:r + 1, :].partition_broadcast(128)), "ssi", 16)
            for i in range(2):
                C.inc(POOL.memset(ZB[i][:], 0.0), "ssz")
                C.inc(POOL.memset(ZC[i][:], 0.0), "ssz")
            C.barrier()
            usv = fm(usS)
            ygv = fm(ygS)
            BsV = BsS.rearrange("d g (v c q) -> d g c v q", v=2, c=16)
            PIECES = [(0, 512), (512, 512), (1024, 512), (1536, 512), (2048, 256)]
            its = [(blk, d, j) for blk in range(16) for d in range(2) for j in range(8)]

            def emit_tables(n):
                blk, d, j = its[n]
                g = blk * 8 + j
                ti = n % 2
                C.wait_evt(POOL, C.free.get("frT"))
                io = iota[:, d, :]
                C.wait_evt(ACT, C.free.get("xxT"))
                ACT.activation(out=xxT[:], in_=io, func=AF.Copy, scale=TH[:, d, g:g + 1])
                ACT.activation(out=t1[:], in_=xxT[:], func=AF.Identity, bias=magP[:, 0:1], scale=1.0)
                C.inc(ACT.activation(out=t1[:], in_=t1[:], func=AF.Identity, bias=magN[:, 0:1], scale=1.0), "sstq")
                C.wait(POOL, "sstq")
                C.inc(POOL.tensor_tensor(out=frT[:], in0=xxT[:], in1=t1[:], op=ALU.subtract), "sstp")
                C.free["xxT"] = C.evt("sstp")
                C.wait(ACT, "sstp")
                C.wait_evt(ACT, C.free.get(("tab", ti)))
                ACT.activation(out=snT[ti][:], in_=frT[:], func=AF.Sin, scale=TWO_PI_S)
                ACT.activation(out=abT[:], in_=frT[:], func=AF.Abs)
                C.inc(ACT.activation(out=csT[ti][:], in_=abT[:], func=AF.Sin, scale=-TWO_PI_S, bias=halfpi[:, 0:1]), "ssta")
                C.free["frT"] = C.evt("ssta")
                return C.evt("ssta")

            tab_ready = {0: emit_tables(0)}
            pending_ro = [None]
            for n, (blk, d, j) in enumerate(its):
                g = blk * 8 + j
                ui = blk % 2
                zi = (blk * 2 + d) % 2
                ti = n % 2
                ai = n % 2
                first = (d == 0 and j == 0)
                last = (d == 1 and j == 7)
                if first:
                    C.wait_evt(POOL, C.free.get(("ust", ui)))
                    C.inc(POOL.dma_start(out=ust[ui][:], in_=usv[:, blk, :]), f"ssul{ui}", 16)
                    C.wait_evt(SP, C.free.get(("usf", ui)))
                    C.inc(SP.dma_start(out=usf[ui][:], in_=usv[:, blk, :]), f"ssvl{ui}", 16)
                if j == 0:
                    C.wait_evt(POOL, C.free.get(("Z", zi)))
                    for jj in range(8):
                        C.inc(POOL.dma_start(out=ZB[zi][16 * jj:16 * (jj + 1), jj, :, :], in_=BsV[d, blk * 8 + jj]), f"sszl{zi}", 16)
                    C.wait_evt(ACT, C.free.get(("Z", zi)))
                    for jj in range(8):
                        ins = ACT.activation(out=ZC[zi][:, jj, :, 16 * jj:16 * (jj + 1)], in_=CM[:, d, blk * 8 + jj, :, :], func=AF.Copy)
                    C.inc(ins, f"sszc{zi}")
                if n + 1 < len(its):
                    tab_ready[n + 1] = emit_tables(n + 1)
                C.wait_evt(DVE, tab_ready.pop(n))
                C.wait(PE, f"ssul{ui}")
                C.wait(PE, f"sszl{zi}")
                for (p0, PL) in PIECES:
                    C.bank_begin(5)
                    C.bank_ready(PE.matmul(ps[5][:, :PL], ZB[zi][:, j, 0, :], ust[ui][:, p0:p0 + PL], start=True, stop=True), 5)
                    C.bank_begin(6)
                    C.bank_ready(PE.matmul(ps[6][:, :PL], ZB[zi][:, j, 1, :], ust[ui][:, p0:p0 + PL], start=True, stop=True), 6)
                    C.bank_wait(DVE, 5)
                    C.bank_wait(DVE, 6)
                    C.bank_free(DVE.tensor_tensor(out=wT[:, p0:p0 + PL], in0=ps[5][:, :PL], in1=csT[ti][:, p0:p0 + PL], op=ALU.mult), 5)
                    C.bank_free(DVE.tensor_tensor(out=tmpT[:, :PL], in0=ps[6][:, :PL], in1=snT[ti][:, p0:p0 + PL], op=ALU.mult), 6)
                    ins = DVE.tensor_tensor(out=wT[:, p0:p0 + PL], in0=wT[:, p0:p0 + PL], in1=tmpT[:, :PL], op=ALU.add)
                C.inc(ins, "ssw")
                if pending_ro[0] is not None:
                    pending_ro[0]()
                    pending_ro[0] = None
                C.wait(DVE, "ssw")
                C.wait_evt(DVE, C.free.get("gT"))
                rb = RT[:, d, g:g + 1]
                if d == 0:
                    C.inc(DVE.tensor_tensor_scan(out=gT[:, LX:NT], data0=rb.to_broadcast([128, LC]), data1=wT[:, LX:NT], initial=0.0, op0=ALU.mult, op1=ALU.add), "ssw")
                    C.wait(DVE, "ssw")
                    C.inc(DVE.tensor_tensor_scan(out=gT[:, 0:LX], data0=rb.to_broadcast([128, LX]), data1=wT[:, 0:LX], initial=gT[:, NT - 1:NT], op0=ALU.mult, op1=ALU.add), "ssw")
                else:
                    C.inc(DVE.tensor_tensor_scan(out=gT[:, LX:NT][:, ::-1], data0=rb.to_broadcast([128, LC]), data1=wT[:, LX:NT][:, ::-1], initial=0.0, op0=ALU.mult, op1=ALU.add), "ssw")
                    C.wait(DVE, "ssw")
                    C.inc(DVE.tensor_tensor_scan(out=gT[:, 0:LX][:, ::-1], data0=rb.to_broadcast([128, LX]), data1=wT[:, 0:LX][:, ::-1], initial=gT[:, LX:LX + 1], op0=ALU.mult, op1=ALU.add), "ssw")
                C.wait(POOL, "ssw")
                C.wait_evt(POOL, C.free.get(("A", ai)))
                POOL.tensor_tensor(out=A1[ai][:], in0=gT[:], in1=csT[ti][:], op=ALU.mult)
                C.inc(POOL.tensor_tensor(out=A2[ai][:], in0=gT[:], in1=snT[ti][:], op=ALU.mult), f"ssal{ai}")
                C.free[("tab", ti)] = C.evt(f"ssal{ai}")
                C.free["gT"] = C.evt(f"ssal{ai}")
                ev_al = C.evt(f"ssal{ai}")
                ev_zc = C.evt(f"sszc{zi}")

                def ro(blk=blk, d=d, j=j, ui=ui, zi=zi, ai=ai, first=first, last=last, ev_al=ev_al, ev_zc=ev_zc):
                    C.wait_evt(PE, ev_al)
                    C.wait_evt(PE, ev_zc)
                    for pi, (p0, PL) in enumerate(PIECES):
                        if first:
                            C.bank_begin(pi)
                        PE.matmul(ps[pi][:, :PL], ZC[zi][:, j, 0, :], A1[ai][:, p0:p0 + PL], start=first, stop=False)
                        ins = PE.matmul(ps[pi][:, :PL], ZC[zi][:, j, 1, :], A2[ai][:, p0:p0 + PL], start=False, stop=last)
                        if last:
                            C.bank_ready(ins, pi)
                    ev = pe_mark("ssam")
                    C.free[("A", ai)] = ev
                    if j == 7:
                        C.free[("Z", zi)] = ev
                    if last:
                        C.free[("ust", ui)] = ev
                        yi = blk % 2
                        C.wait(DVE, f"ssvl{ui}")
                        C.wait(ACT, f"ssygs{yi}", 16 * (blk // 2))
                        for pi, (p0, PL) in enumerate(PIECES):
                            C.bank_wait(DVE, pi)
                            C.wait(DVE, "ssge")
                            ins = DVE.scalar_tensor_tensor(out=ytmp[:, :PL], in0=usf[ui][:, p0:p0 + PL], scalar=Dv[:, blk:blk + 1], in1=ps[pi][:, :PL], op0=ALU.mult, op1=ALU.add)
                            C.bank_free(ins, pi)
                            C.wait(ACT, f"bf{pi}")
                            C.inc(ACT.activation(out=ygs[yi][:, p0:p0 + PL], in_=ytmp[:, :PL], func=AF.Gelu_apprx_tanh), "ssge")
                        C.free[("usf", ui)] = C.evt("ssge")
                        C.wait(ACT, "ssge")
                        C.inc(ACT.dma_start(out=ygv[:, blk, :], in_=ygs[yi][:]), f"ssygs{yi}", 16)
                pending_ro[0] = ro
            pending_ro[0]()
            C.barrier()

    def phase_glu(l, tiles=TILES):
        with ExitStack() as ph:
            act = sb(ph, "gl_act", [128, 16, NT], BF16)
            wb = [sb(ph, f"gl_wb{i}", [128, 16, 512], BF16) for i in range(2)]
            sbf = Stage(ph, "glsb", 3, [128, 512], BF16)
            gl = Loader(ph, "glg", 3, [128, 512], BF16)
            ml = Loader(ph, "glm", 3, [128, 512], F32)
            sg = [sb(ph, f"gl_sg{i}", [128, 512], F32) for i in range(2)]
            dq = Deferred()
            ygv = fm(ygS)
            for q in range(2):
                C.inc(SP.dma_start(out=act[:, q * 8:(q + 1) * 8, :], in_=ygv[:, q * 8:(q + 1) * 8, :]), "actl", 16)
            C.wait(PE, "actl")
            wv = W["glu_w"][l].rearrange("(k p) n -> p k n", p=128)
            gtv, m1v, mgv = fm(gtS), fm(m1S), fm(mgS)
            bsel = 0
            for j in range(D // 256):
                i = j % 2
                C.wait_evt(POOL, C.free.get(("glw", i)))
                C.inc(POOL.dma_start(out=wb[i][:, :, 0:256], in_=wv[:, :, j * 256:(j + 1) * 256]), f"wl{i}", 16)
                C.inc(POOL.dma_start(out=wb[i][:, :, 256:512], in_=wv[:, :, D + j * 256:D + (j + 1) * 256]), f"wl{i}", 16)
                C.wait(PE, f"wl{i}")
                for sub in range(2):
                    fc = j * 2 + sub
                    for ti, (t0, T, r) in enumerate(tiles):
                        bv = (bsel % 3) * 2
                        bg = bv + 1
                        si_ = bsel % 2
                        bsel += 1
                        C.bank_begin(bv)
                        for kc in range(16):
                            ins = PE.matmul(ps[bv][:, :T], wb[i][:, kc, sub * 128:(sub + 1) * 128], act[:, kc, t0:t0 + T], start=(kc == 0), stop=(kc == 15))
                        C.bank_ready(ins, bv)
                        C.bank_begin(bg)
                        for kc in range(16):
                            ins = PE.matmul(ps[bg][:, :T], wb[i][:, kc, 256 + sub * 128:256 + (sub + 1) * 128], act[:, kc, t0:t0 + T], start=(kc == 0), stop=(kc == 15))
                        C.bank_ready(ins, bg)
                        C.free[("glw", i)] = C.bank_evt(bg)
                        gl.issue(lambda bf: bf[:, :T], gtv[:, 32 + fc, t0:t0 + T])
                        ml.issue(lambda bf: bf[:, :T], m1v[:, fc, t0:t0 + T])
                        C.bank_wait(ACT, bg)
                        C.wait_evt(ACT, C.free.get(("glsg", si_)))
                        ins = ACT.activation(out=sg[si_][:, :T], in_=ps[bg][:, :T], func=AF.Sigmoid)
                        C.bank_free(ins, bg)
                        dq.flush()
                        C.wait(DVE, f"bf{bg}")
                        C.bank_wait(DVE, bv)
                        ins = DVE.tensor_tensor(out=sg[si_][:, :T], in0=ps[bv][:, :T], in1=sg[si_][:, :T], op=ALU.mult)
                        C.bank_free(ins, bv)
                        gi, gb = gl.take(DVE)
                        mi, mb = ml.take(DVE)
                        DVE.tensor_tensor(out=sg[si_][:, :T], in0=sg[si_][:, :T], in1=gb[:, :T], op=ALU.mult)
                        sti, st = sbf.get(DVE)
                        ins = DVE.tensor_tensor(out=st[:, :T], in0=sg[si_][:, :T], in1=mb[:, :T], op=ALU.add)
                        C.inc(ins, "ep")
                        ev = C.evt("ep")
                        gl.release(gi, ev)
                        ml.release(mi, ev)
                        C.free[("glsg", si_)] = ev

                        def store(ev=ev, dst=mgv[:, fc, t0:t0 + T], src=st[:, :T], nm=sbf.stsem(sti)):
                            C.wait_evt(ACT, ev)
                            C.inc(ACT.dma_start(out=dst, in_=src), nm, 16)
                        dq.push(store)
            dq.flush()
            C.barrier()

    def make_resid_epi(ph, l, gate_m, name):
        sf = Stage(ph, f"{name}sf", 2, [128, 512], F32)
        xl = Loader(ph, f"{name}x", 2, [128, 512], F32)
        tb = [sb(ph, f"{name}_t{i}", [128, 512], F32) for i in range(2)]
        xTv, prev = fm(xT), fm(pre)
        cnt = [0]
        dq = Deferred()

        def epi(fc, ti, tl, b):
            t0, T, r = tl
            n = cnt[0]
            cnt[0] += 1
            i2 = n % 2
            xl.issue(lambda bf: bf[:, :T], xTv[:, fc, t0:t0 + T])
            C.bank_wait(ACT, b)
            C.wait_evt(ACT, C.free.get((name, "tb", i2)))
            ins = ACT.activation(out=tb[i2][:, :T], in_=ps[b][:, :T], func=AF.Copy, scale=mods[l][:, gate_m * 32 + fc, r:r + 1])
            C.bank_free(ins, b)
            dq.flush()
            C.wait(DVE, f"bf{b}")
            xi, xbuf = xl.take(DVE)
            si, st = sf.get(DVE)
            ins = DVE.scalar_tensor_tensor(out=st[:, :T], in0=xbuf[:, :T], scalar=ALPHA, in1=tb[i2][:, :T], op0=ALU.mult, op1=ALU.add)
            C.inc(ins, "ep")
            ev = C.evt("ep")
            xl.release(xi, ev)
            C.free[(name, "tb", i2)] = ev

            def store(ev=ev, dst=prev[:, fc, t0:t0 + T], src=st[:, :T], nm=sf.stsem(si)):
                C.wait_evt(ACT, ev)
                C.inc(ACT.dma_start(out=dst, in_=src), nm, 16)
            dq.push(store)
        return epi, dq

    def phase_wout(l, tiles=TILES):
        with ExitStack() as ph:
            act = sb(ph, "wo_act", [128, 32, NT], BF16)
            wb = [sb(ph, f"wo_wb{i}", [128, 32, 256], BF16) for i in range(2)]
            mgv = fm(mgS)
            for q in range(4):
                C.inc(SP.dma_start(out=act[:, q * 8:(q + 1) * 8, :], in_=mgv[:, q * 8:(q + 1) * 8, :]), "actl", 16)
            C.wait(PE, "actl")
            wv = W["w_out"][l].rearrange("(k p) n -> p k n", p=128)
            epi, dq = make_resid_epi(ph, l, 2, "wo")

            def wview(j, buf):
                return [(buf[:], wv[:, :, j * 256:(j + 1) * 256])]
            gemm("wo", act, 32, wview, D // 256, 256, epi, wbufs=wb, ntiles=tiles)
            dq.flush()
            C.barrier()

    def phase_ffn(l, tiles=TILES):
        with ExitStack() as ph:
            h2 = sb(ph, "ff_h2", [128, 32, 512], BF16)
            actT = sb(ph, "ff_act", [128, HB, 512], BF16)
            wu = [sb(ph, f"ff_wu{i}", [128, 32, 256], BF16) for i in range(2)]
            wvb = [sb(ph, f"ff_wv{i}", [128, 32, 256], BF16) for i in range(2)]
            cw = sb(ph, "ff_cw", [128, 3, HB], F32)
            cbias = sb(ph, "ff_cb", [128, HB], F32)
            cbuf = [sb(ph, f"ff_c{i}", [128, 512], F32) for i in range(2)]
            for kk in range(3):
                load_vec_fm(cw[:, kk, :], W["ffn_conv_w"][l, kk], HB)
            load_vec_fm(cbias[:, :], W["ffn_conv_b"][l], HB)
            C.barrier()
            hTv = fm(hT)
            w12 = W["ffn_w12"][l].rearrange("(k p) n -> p k n", p=128)
            w2 = W["ffn_w2"][l].rearrange("(k p) n -> p k n", p=128)
            epi2, dq = make_resid_epi(ph, l, 5, "ff")
            w2v = [wu[i][:].rearrange("p a b -> p (a b)")[:, 0:43 * 128].rearrange("p (a b) -> p a b", b=128) for i in range(2)]
            ec = 0
            for ti, (t0, T, r) in enumerate(tiles):
                C.wait_evt(SP, C.free.get("ffh2"))
                C.inc(SP.dma_start(out=h2[:, :, :T], in_=hTv[:, :, t0:t0 + T]), "actl", 16)
                C.wait(PE, "actl")
                rows = T // 64 if r == 0 else 1
                RL = T // rows
                for j in range(FFN // 256):
                    i = j % 2
                    C.wait_evt(POOL, C.free.get(("ffwu", i)))
                    C.inc(POOL.dma_start(out=wu[i][:], in_=w12[:, :, j * 256:(j + 1) * 256]), f"wl{i}", 16)
                    C.wait_evt(POOL, C.free.get(("ffwv", i)))
                    C.inc(POOL.dma_start(out=wvb[i][:], in_=w12[:, :, FFN + j * 256:FFN + (j + 1) * 256]), f"wl{i}", 16)
                    C.wait(PE, f"wl{i}")
                    for sub in range(2):
                        hb = j * 2 + sub
                        bu = (ec % 2) * 2
                        bv = bu + 1
                        ci = ec % 2
                        ec += 1
                        C.bank_begin(bu)
                        for kc in range(32):
                            ins = PE.matmul(ps[bu][:, :T], wu[i][:, kc, sub * 128:(sub + 1) * 128], h2[:, kc, :T], start=(kc == 0), stop=(kc == 31))
                        C.bank_ready(ins, bu)
                        C.free[("ffwu", i)] = C.bank_evt(bu)
                        C.bank_begin(bv)
                        for kc in range(32):
                            ins = PE.matmul(ps[bv][:, :T], wvb[i][:, kc, sub * 128:(sub + 1) * 128], h2[:, kc, :T], start=(kc == 0), stop=(kc == 31))
                        C.bank_ready(ins, bv)
                        C.free[("ffwv", i)] = C.bank_evt(bv)
                        C.free["ffh2"] = C.bank_evt(bv)
                        C.bank_wait(ACT, bu)
                        C.wait_evt(ACT, C.free.get(("ffg", ci)))
                        C.inc(ACT.activation(out=cbuf[ci][:, :T], in_=ps[bu][:, :T], func=AF.Identity,
                                             bias=cbias[:, hb:hb + 1], scale=cw[:, 1, hb:hb + 1]), "ffc1")
                        C.wait(DVE, "ffc1")
                        c3 = cbuf[ci][:, :T].rearrange("p (a w) -> p a w", w=RL)
                        u3 = ps[bu][:, :T].rearrange("p (a w) -> p a w", w=RL)
                        DVE.scalar_tensor_tensor(out=c3[:, :, 1:RL], in0=u3[:, :, 0:RL - 1], scalar=cw[:, 0, hb:hb + 1], in1=c3[:, :, 1:RL], op0=ALU.mult, op1=ALU.add)
                        ins = DVE.scalar_tensor_tensor(out=c3[:, :, 0:RL - 1], in0=u3[:, :, 1:RL], scalar=cw[:, 2, hb:hb + 1], in1=c3[:, :, 0:RL - 1], op0=ALU.mult, op1=ALU.add)
                        C.bank_free(ins, bu)
                        C.wait(ACT, f"bf{bu}")
                        C.inc(ACT.activation(out=cbuf[ci][:, :T], in_=cbuf[ci][:, :T], func=AF.Gelu_apprx_tanh), "ffc2")
                        C.wait(DVE, "ffc2")
                        C.bank_wait(DVE, bv)
                        C.wait_evt(DVE, C.free.get("ffact"))
                        ins = DVE.tensor_tensor(out=actT[:, hb, :T], in0=cbuf[ci][:, :T], in1=ps[bv][:, :T], op=ALU.mult)
                        C.bank_free(ins, bv)
                        C.free[("ffg", ci)] = C.evt(f"bf{bv}")
                        act_done = C.evt(f"bf{bv}")
                C.wait_evt(PE, act_done)
                for fc in range(32):
                    b = 4 + (fc % 3)
                    C.bank_begin(b)
                    for half in range(2):
                        i = half
                        C.wait_evt(POOL, C.free.get(("ffwu", i)))
                        C.inc(POOL.dma_start(out=w2v[i], in_=w2[:, half * 43:(half + 1) * 43, fc * 128:(fc + 1) * 128]), f"wl{i}", 16)
                        C.wait(PE, f"wl{i}")
                        for kk in range(43):
                            kc = half * 43 + kk
                            ins = PE.matmul(ps[b][:, :T], w2v[i][:, kk, :], actT[:, kc, :T], start=(kc == 0), stop=(kc == HB - 1))
                        if half == 0:
                            C.inc(ins, "ffw2h")
                            C.free[("ffwu", i)] = C.evt("ffw2h")
                    C.bank_ready(ins, b)
                    C.free[("ffwu", 1)] = C.bank_evt(b)
                    C.free["ffact"] = C.bank_evt(b)
                    epi2(fc, ti, (t0, T, r), b)
            dq.flush()
            C.barrier()

    for l in range(DEPTH):
        phase_mods(l)
    if stop_after == "mods":
        return nc, C
    phase_t0()
    if stop_after == "t0":
        return nc, C
    for l in range(DEPTH):
        if l == 0:
            ln_pass(l, xT, "mod1")
            if stop_after == "ln0":
                return nc, C
        phase_win(l)
        if stop_after == "win":
            return nc, C
        phase_fourier_a(l)
        if stop_after == "fa":
            return nc, C
        tl = TILES[:4] if l == DEPTH - 1 else TILES
        phase_fourier_b(l, tl)
        if stop_after == "fb":
            return nc, C
        phase_ssm(l)
        if stop_after == "ssm":
            return nc, C
        phase_glu(l, tl)
        if stop_after == "glu":
            return nc, C
        phase_wout(l, tl)
        if stop_after == "wout":
            return nc, C
        ln_pass(l, pre, "post1", tl)
        if stop_after == "post1":
            return nc, C
        phase_ffn(l, tl)
        if stop_after == "ffn":
            return nc, C
        ln_pass(l, pre, "post2" if l + 1 < DEPTH else "final", tl)
        if stop_after == "post2":
            return nc, C
    return nc, C


def host_consts():
    l = np.arange(2048, dtype=np.int64)
    prod = (l[:, None] * l[None, :]) % 2048
    ang = prod.astype(np.float64) * (2.0 * np.pi / 2048.0)
    cosT = np.cos(ang).astype(np.float32)
    sinT = np.sin(ang).astype(np.float32)
    q = np.arange(NT)
    iota_f = np.where(q < LX, q + LC, q - LX).astype(np.float32)
    iota_b = (NT - 1 - q).astype(np.float32)
    iota2 = np.stack([iota_f, iota_b]).astype(np.float32)
    return cosT, sinT, iota2, np.eye(128, dtype=np.float32)


def make_in_maps(inputs, cores):
    cosT, sinT, iota2, ident = host_consts()
    maps = []
    for c in cores:
        b = c % 4
        m = {
            "xb": np.ascontiguousarray(inputs["x"][b]),
            "ctxb": np.ascontiguousarray(inputs["ctx"][b]),
            "cvec": np.ascontiguousarray(np.stack([inputs["c"][b], inputs["c_ctx"]])),
            "ident_in": ident, "cosT": cosT, "sinT": sinT, "iota2": iota2,
        }
        for k in WSHAPES:
            m[k] = np.ascontiguousarray(inputs[k])
        maps.append(m)
    return maps


def kernel(**inputs):
    inputs = {k: np.asarray(v, dtype=np.float32) for k, v in inputs.items()}
    nc, C = build()
    cores = list(range(8))
    res = run_bass_kernel_spmd(nc, make_in_maps(inputs, cores), core_ids=cores)
    out = np.stack([res.results[b]["out"] for b in range(4)], axis=0)
    return out.astype(np.float32)
```
